# Optimizing a Trainium2 kernel written in Bass

```python
import math
import jax
import jax.numpy as jnp
from jax import lax
import numpy as np

D_MODEL = 2048
BATCH = 4
SEQ = 4096
DEPTH = 2

N_A_LAYERS = DEPTH // 2
N_B_LAYERS = DEPTH - N_A_LAYERS
EPS = 1e-6

SSM_EXPAND = 2
D_INNER = SSM_EXPAND * D_MODEL
SSM_HEAD_DIM = 64
SSM_HEADS = D_INNER // SSM_HEAD_DIM
SSM_GROUPS = 8
SSM_HEADS_PER_GROUP = SSM_HEADS // SSM_GROUPS
D_STATE = 128
CONV_WIDTH = 4
SSD_CHUNK = 128
D_BC = SSM_GROUPS * D_STATE
D_XBC = D_INNER + 2 * D_BC
D_IN_PROJ = D_INNER + D_XBC + SSM_HEADS

ATT_HEAD_DIM = 128
ATT_HEADS = D_MODEL // ATT_HEAD_DIM
D_ATT = ATT_HEADS * ATT_HEAD_DIM
Q_BLOCK = 128

FFN_MULT_OF = 256
D_FF = -(-8 * D_MODEL // (3 * FFN_MULT_OF)) * FFN_MULT_OF

kernel_name = 'yoco_mamba2_fox_hybrid'


def rmsnorm(x, w):
    xf = x.astype(jnp.float32)
    y = xf * lax.rsqrt(jnp.mean(xf * xf, axis=-1, keepdims=True) + EPS)
    return (y * w.astype(jnp.float32)).astype(x.dtype)


def causal_depthwise_conv(u, w, b):
    c = u.shape[-1]
    out = lax.conv_general_dilated(
        u, w[:, None, :].astype(u.dtype), window_strides=(1,),
        padding=[(CONV_WIDTH - 1, 0)], dimension_numbers=('NWC', 'WIO', 'NWC'),
        feature_group_count=c)
    return out + b.astype(u.dtype)


def ssd_chunked_scan(xdt, a, bm, cm):
    bsz, seq = xdt.shape[:2]
    nc = seq // SSD_CHUNK
    g, kh, p = SSM_GROUPS, SSM_HEADS_PER_GROUP, SSM_HEAD_DIM
    f32 = jnp.float32

    def chunks(t):
        t = t.reshape((bsz, nc, SSD_CHUNK) + t.shape[2:])
        return jnp.moveaxis(t, 1, 0)

    xc = chunks(xdt.astype(f32).reshape(bsz, seq, g, kh, p))
    ac = chunks(a.astype(f32).reshape(bsz, seq, g, kh))
    bc = chunks(bm.astype(f32))
    cc = chunks(cm.astype(f32))
    causal = jnp.tril(jnp.ones((SSD_CHUNK, SSD_CHUNK), dtype=bool))[None, :, :, None, None]

    def step(state, inp):
        x_, a_, b_, c_ = inp
        a_cum = jnp.cumsum(a_, axis=1)
        seg = a_cum[:, :, None] - a_cum[:, None, :]
        decay = jnp.exp(jnp.where(causal, seg, -jnp.inf))
        cb = jnp.einsum('btgn,bsgn->btsg', c_, b_)
        y_diag = jnp.einsum('btsg,btsgk,bsgkp->btgkp', cb, decay, x_)
        y_off = jnp.einsum('btgn,bgkpn->btgkp', c_, state) * jnp.exp(a_cum)[..., None]
        a_last = a_cum[:, -1]
        w_in = jnp.exp(a_last[:, None] - a_cum)
        new_state = state * jnp.exp(a_last)[..., None, None] + jnp.einsum('bsgn,bsgk,bsgkp->bgkpn', b_, w_in, x_)
        return new_state, y_diag + y_off

    state0 = jnp.zeros((bsz, g, kh, p, D_STATE), f32)
    _, y = lax.scan(step, state0, (xc, ac, bc, cc))
    return jnp.moveaxis(y, 0, 1).reshape(bsz, seq, SSM_HEADS, p)


def mamba2_mixer(h, in_proj, conv_w, conv_b, dt_bias, a_log, d_skip, gnorm_w, out_proj):
    bsz, seq, _ = h.shape
    f32 = jnp.float32
    zxbcdt = h @ in_proj
    z, xbc, dt_raw = jnp.split(zxbcdt, [D_INNER, D_INNER + D_XBC], axis=-1)
    xbc = jax.nn.silu(causal_depthwise_conv(xbc, conv_w, conv_b))
    xs, bm, cm = jnp.split(xbc, [D_INNER, D_INNER + D_BC], axis=-1)
    xs = xs.astype(f32).reshape(bsz, seq, SSM_HEADS, SSM_HEAD_DIM)
    bm = bm.reshape(bsz, seq, SSM_GROUPS, D_STATE)
    cm = cm.reshape(bsz, seq, SSM_GROUPS, D_STATE)
    dt = jax.nn.softplus(dt_raw.astype(f32) + dt_bias.astype(f32))
    a_neg = -jnp.exp(a_log.astype(f32))
    y = ssd_chunked_scan(xs * dt[..., None], dt * a_neg, bm, cm)
    y = y + xs * d_skip.astype(f32)[:, None]
    y = y.reshape(bsz, seq, D_INNER) * jax.nn.silu(z.astype(f32))
    yg = y.reshape(bsz, seq, SSM_GROUPS, D_INNER // SSM_GROUPS)
    yg = yg * lax.rsqrt(jnp.mean(yg * yg, axis=-1, keepdims=True) + EPS)
    y = (yg.reshape(bsz, seq, D_INNER) * gnorm_w.astype(f32)).astype(h.dtype)
    return y @ out_proj


def shared_kv(s, kv_norm_w, w_kvf, b_f, k_norm_w):
    bsz, seq, _ = s.shape
    kvf = rmsnorm(s, kv_norm_w) @ w_kvf
    k, v, f_logit = jnp.split(kvf, [D_ATT, 2 * D_ATT], axis=-1)
    k = rmsnorm(k.reshape(bsz, seq, ATT_HEADS, ATT_HEAD_DIM), k_norm_w)
    v = v.reshape(bsz, seq, ATT_HEADS, ATT_HEAD_DIM)
    log_f = jax.nn.log_sigmoid(f_logit.astype(jnp.float32) + b_f.astype(jnp.float32))
    cum = jnp.cumsum(log_f, axis=1)
    return k, v, cum


def forgetting_attention(h, k, v, cum, w_q, q_norm_w, w_o):
    bsz, seq, _ = h.shape
    q = rmsnorm((h @ w_q).reshape(bsz, seq, ATT_HEADS, ATT_HEAD_DIM), q_norm_w)
    scale = ATT_HEAD_DIM ** -0.5
    cum_h = jnp.swapaxes(cum, 1, 2)
    outs = []
    for blk in range(seq // Q_BLOCK):
        q0 = blk * Q_BLOCK
        kend = q0 + Q_BLOCK
        logits = jnp.einsum('bthd,bshd->bhts', q[:, q0:kend], k[:, :kend]).astype(jnp.float32) * scale
        logits = logits + (cum_h[:, :, q0:kend, None] - cum_h[:, :, None, :kend])
        mask = jnp.arange(kend)[None, :] <= (q0 + jnp.arange(Q_BLOCK))[:, None]
        logits = jnp.where(mask, logits, -jnp.inf)
        probs = jax.nn.softmax(logits, axis=-1).astype(v.dtype)
        outs.append(jnp.einsum('bhts,bshd->bthd', probs, v[:, :kend]))
    o = jnp.concatenate(outs, axis=1).reshape(bsz, seq, D_ATT)
    return o @ w_o


def swiglu(h, w_gate_up, w_down):
    g, u = jnp.split(h @ w_gate_up, 2, axis=-1)
    return (jax.nn.silu(g) * u) @ w_down


def setup_inputs(seed: int = 0) -> dict:
    key = jax.random.key(seed)
    ks = jax.random.split(key, 24)
    f32 = jnp.float32

    def nrm(k, shape, scale):
        return jax.random.normal(k, shape, f32) * scale

    def gain(k, shape):
        return 1.0 + 0.02 * jax.random.normal(k, shape, f32)

    x = nrm(ks[0], (BATCH, SEQ, D_MODEL), 1.0)
    a_norm_w = gain(ks[1], (N_A_LAYERS, D_MODEL))
    a_in_proj = nrm(ks[2], (N_A_LAYERS, D_MODEL, D_IN_PROJ), D_MODEL ** -0.5)
    a_conv_w = nrm(ks[3], (N_A_LAYERS, CONV_WIDTH, D_XBC), CONV_WIDTH ** -0.5)
    a_conv_b = nrm(ks[4], (N_A_LAYERS, D_XBC), 0.02)
    dt0 = jnp.exp(jax.random.uniform(ks[5], (N_A_LAYERS, SSM_HEADS), f32, math.log(1e-3), math.log(1e-1)))
    a_dt_bias = dt0 + jnp.log(-jnp.expm1(-dt0))
    a_A_log = jnp.log(jax.random.uniform(ks[6], (N_A_LAYERS, SSM_HEADS), f32, 1.0, 16.0))
    a_D = 1.0 + 0.1 * jax.random.normal(ks[7], (N_A_LAYERS, SSM_HEADS), f32)
    a_gnorm_w = gain(ks[8], (N_A_LAYERS, D_INNER))
    a_out_proj = nrm(ks[9], (N_A_LAYERS, D_INNER, D_MODEL), D_INNER ** -0.5)
    kv_norm_w = gain(ks[10], (D_MODEL,))
    w_kvf = nrm(ks[11], (D_MODEL, 2 * D_ATT + ATT_HEADS), D_MODEL ** -0.5)
    b_f = jax.random.uniform(ks[12], (ATT_HEADS,), f32, 1.0, 4.0)
    k_norm_w = gain(ks[13], (ATT_HEAD_DIM,))
    b_norm_w = gain(ks[14], (N_B_LAYERS, D_MODEL))
    w_q = nrm(ks[15], (N_B_LAYERS, D_MODEL, D_ATT), D_MODEL ** -0.5)
    q_norm_w = gain(ks[16], (N_B_LAYERS, ATT_HEAD_DIM))
    w_o = nrm(ks[17], (N_B_LAYERS, D_ATT, D_MODEL), D_ATT ** -0.5)
    ffn_norm_w = gain(ks[18], (DEPTH, D_MODEL))
    w_gate_up = nrm(ks[19], (DEPTH, D_MODEL, 2 * D_FF), D_MODEL ** -0.5)
    w_down = nrm(ks[20], (DEPTH, D_FF, D_MODEL), D_FF ** -0.5)
    return {'x': x, 'a_norm_w': a_norm_w, 'a_in_proj': a_in_proj, 'a_conv_w': a_conv_w,
            'a_conv_b': a_conv_b, 'a_dt_bias': a_dt_bias, 'a_A_log': a_A_log, 'a_D': a_D,
            'a_gnorm_w': a_gnorm_w, 'a_out_proj': a_out_proj, 'kv_norm_w': kv_norm_w,
            'w_kvf': w_kvf, 'b_f': b_f, 'k_norm_w': k_norm_w, 'b_norm_w': b_norm_w,
            'w_q': w_q, 'q_norm_w': q_norm_w, 'w_o': w_o, 'ffn_norm_w': ffn_norm_w,
            'w_gate_up': w_gate_up, 'w_down': w_down}


def reference(x, a_norm_w, a_in_proj, a_conv_w, a_conv_b, a_dt_bias, a_A_log, a_D,
              a_gnorm_w, a_out_proj, kv_norm_w, w_kvf, b_f, k_norm_w, b_norm_w,
              w_q, q_norm_w, w_o, ffn_norm_w, w_gate_up, w_down):
    h = x
    k_sh = v_sh = cum_sh = None
    for layer in range(DEPTH):
        if layer < N_A_LAYERS:
            i = layer
            h = h + mamba2_mixer(rmsnorm(h, a_norm_w[i]), a_in_proj[i], a_conv_w[i], a_conv_b[i],
                                 a_dt_bias[i], a_A_log[i], a_D[i], a_gnorm_w[i], a_out_proj[i])
        else:
            j = layer - N_A_LAYERS
            if j == 0:
                k_sh, v_sh, cum_sh = shared_kv(h, kv_norm_w, w_kvf, b_f, k_norm_w)
            h = h + forgetting_attention(rmsnorm(h, b_norm_w[j]), k_sh, v_sh, cum_sh,
                                         w_q[j], q_norm_w[j], w_o[j])
        h = h + swiglu(rmsnorm(h, ffn_norm_w[layer]), w_gate_up[layer], w_down[layer])
    return h
```

```python
import contextlib
import numpy as np
import concourse.bass as bass
import concourse.mybir as mybir
from concourse.bass_utils import run_bass_kernel_spmd

F32 = mybir.dt.float32
F32R = mybir.dt.float32r
AF = mybir.ActivationFunctionType
ALU = mybir.AluOpType
AX = mybir.AxisListType

D = 2048
DI = 4096
NH = 64
HP = 64
NG = 8
NS = 128
DXBC = 6144
DINP = 10304
DFF = 5632
AH = 16
AD = 128
T = 512
KC = D // 128
EPS = 1e-6
NEG = -30000.0


class Buf:
    __slots__ = ("name", "w", "r")

    def __init__(self, name=""):
        self.name = name
        self.w = []
        self.r = {}


class Tl:
    __slots__ = ("ap", "bufs")

    def __init__(self, ap, bufs=None, name=""):
        self.ap = ap
        self.bufs = bufs if bufs is not None else [Buf(name)]


class Op:
    __slots__ = ("eng", "fn", "deps", "idx", "sig", "sem", "semval", "dma", "stream", "inc")


class Prog:
    ENGS = ("pe", "act", "dve", "pool", "sp")

    def __init__(self, nc):
        self.nc = nc
        self.ops = []
        self.streams = {}

    def add(self, eng, fn, reads=(), writes=(), stream=None, accum=False, inc=16):
        op = Op()
        op.inc = inc
        op.eng, op.fn, op.idx = eng, fn, len(self.ops)
        op.dma = stream is not None
        op.stream = stream
        op.sig = op.dma
        deps = set()
        rb = [b for t in reads for b in t.bufs]
        wb = [b for t in writes for b in t.bufs]
        for b in rb:
            deps.update(b.w)
        for b in wb:
            deps.update(b.w)
            deps.update(b.r.values())
        op.deps = deps
        rkey = (eng, stream) if stream is not None else eng
        for b in rb:
            b.r[rkey] = op.idx
        for b in wb:
            if accum:
                b.w.append(op.idx)
            else:
                b.w = [op.idx]
            b.r = {}
        if op.dma:
            st = self.streams.setdefault(stream, {"count": 0})
            st["count"] += inc
            op.semval = st["count"]
        self.ops.append(op)
        return op

    def emit(self):
        nc = self.nc
        ops = self.ops
        for op in ops:
            for d in op.deps:
                dop = ops[d]
                if dop.dma:
                    continue
                if dop.eng == op.eng and op.eng == "pe" and not op.dma:
                    continue
                dop.sig = True
        with contextlib.ExitStack() as es:
            esem = {e: es.enter_context(nc.semaphore("s_" + e)) for e in self.ENGS}
            ssem = {s: es.enter_context(nc.semaphore("d_" + s)) for s in self.streams}
            cnt = {e: 0 for e in self.ENGS}
            for op in ops:
                if op.dma:
                    op.sem = ssem[op.stream]
                    if op.stream.startswith("const"):
                        op.semval = self.streams[op.stream]["count"]
                else:
                    op.sem = esem[op.eng]
                    if op.sig:
                        cnt[op.eng] += 1
                        op.semval = cnt[op.eng]
            block = es.enter_context(nc.Block())
            per = {e: [o for o in ops if o.eng == e] for e in self.ENGS}
            final_waits = [(ssem[s], st["count"]) for s, st in self.streams.items()]

            def run(e, ename):
                seen = {}
                for op in per[ename]:
                    for d in sorted(op.deps):
                        dop = ops[d]
                        if (not dop.dma) and dop.eng == ename and ename == "pe" and not op.dma:
                            continue
                        key = id(dop.sem)
                        if seen.get(key, 0) >= dop.semval:
                            continue
                        e.wait_ge(dop.sem, dop.semval)
                        seen[key] = dop.semval
                    ins = op.fn(e)
                    if op.sig:
                        ins.then_inc(op.sem, op.inc if op.dma else 1)
                if ename == "sp":
                    for s, v in final_waits:
                        e.wait_ge(s, v)

            @block.tensor
            def _(e):
                run(e, "pe")

            @block.scalar
            def _(e):
                run(e, "act")

            @block.vector
            def _(e):
                run(e, "dve")

            @block.gpsimd
            def _(e):
                run(e, "pool")

            @block.sync
            def _(e):
                run(e, "sp")


class StopBuild(Exception):
    pass


class Builder:
    def ckpt(self, name):
        if self.cfg.get("stop") == name:
            raise StopBuild(name)

    def __init__(self, S, cfg):
        self.S = S
        self.cfg = cfg
        self.NT = S // T
        nc = bass.Bass("TRN2", target_bir_lowering=False)
        self.nc = nc
        self.P = Prog(nc)
        self.dram_in = {}
        self._mk_dram()
        self._mk_sbuf()

    def din(self, name, shape):
        t = self.nc.dram_tensor(name, list(shape), F32, kind="ExternalInput").ap()
        self.dram_in[name] = t
        return t

    def _mk_dram(self):
        S = self.S
        nc = self.nc
        self.xT = self.din("xT", [KC, 128, S])
        self.w_in = self.din("w_in", [D, DINP])
        self.w_out = self.din("w_out", [DI, D])
        self.w_gu = [self.din("w_gu0", [D, 2 * DFF]), self.din("w_gu1", [D, 2 * DFF])]
        self.w_dn = [self.din("w_dn0", [DFF, D]), self.din("w_dn1", [DFF, D])]
        self.w_kvf = self.din("w_kvf", [D, 2 * D + AH])
        self.w_q = self.din("w_q", [D, D])
        self.w_o = self.din("w_o", [D, D])
        self.d_normw = self.din("normw", [128, 5 * KC])
        self.d_gnw = self.din("gnw", [128, DI // 128])
        self.d_kqnw = self.din("kqnw", [128, 2])
        self.d_convw = self.din("convw", [128, 48 * 4])
        self.d_convb = self.din("convb", [128, 48])
        self.d_hv = self.din("hvec", [128, 3 * NH])
        self.d_bf = self.din("bfv", [128, AH])
        self.d_consts = self.din("consts", [128, 3 * 128])
        self.outT = nc.dram_tensor("outT", [KC, 128, S], F32, kind="ExternalOutput").ap()
        self.xpT = self.din("xpT", [KC, 128, S])
        self.d_flags = self.din("flags", [128, 2])
        self.h1T = nc.dram_tensor("h1T", [KC, 128, S], F32).ap()
        self.cumTd = nc.dram_tensor("cumTd", [AH, S], F32).ap()
        self.PKh = (AH // 2) * 128 * T
        self.PVh = (T // 2) * D
        self.PC = T * AH
        mk = lambda n, sz: nc.dram_tensor(n, [sz], F32).ap()
        self.packK = [[mk(f"pK{j}_{i}", self.PKh) for i in range(2)] for j in range(self.NT)]
        self.gathK = [[mk(f"gK{j}_{i}", 2 * self.PKh) for i in range(2)] for j in range(self.NT)]
        self.packV = [[mk(f"pV{j}_{i}", self.PVh) for i in range(2)] for j in range(self.NT)]
        self.gathV = [[mk(f"gV{j}_{i}", 2 * self.PVh) for i in range(2)] for j in range(self.NT)]
        self.packC = [mk(f"pC{j}", self.PC) for j in range(self.NT)]
        self.gathC = [mk(f"gC{j}", 2 * self.PC) for j in range(self.NT)]
        self.b_g = [Tl(None, name=f"g_{j}") for j in range(self.NT)]
        self.qTd = nc.dram_tensor("qTd", [AH, 128, T], F32).ap()
        self.b_q = [Tl(None, name=f"q_{h}") for h in range(AH)]
        nt = self.NT
        self.b_h1 = [Tl(None, name=f"h1_{j}") for j in range(nt)]
        self.b_kt = [Tl(None, name=f"kt_{j}") for j in range(nt)]
        self.b_v = [Tl(None, name=f"v_{j}") for j in range(nt)]
        self.b_cum = [Tl(None, name=f"cum_{j}") for j in range(nt)]
        self.b_cumT = [Tl(None, name=f"cumT_{j}") for j in range(nt)]

    def kt_view(self, j, h, prev):
        flat = (self.gathK if prev else self.packK)[j][h // 8]
        return flat[0:self.PKh].rearrange("(h d t) -> h d t", h=AH // 2, d=128)[h % 8]

    def v_view(self, j, vh, prev):
        flat = (self.gathV if prev else self.packV)[j][vh]
        return flat[0:self.PVh].rearrange("(t c) -> t c", c=D)

    def cum_view(self, j, prev):
        flat = (self.gathC if prev else self.packC)[j]
        return flat[0:self.PC].rearrange("(t h) -> t h", h=AH)

    def sb(self, name, shape, dt=F32):
        return self.nc.alloc_sbuf_tensor("sb_" + name, list(shape), dt).ap()

    def _mk_sbuf(self):
        nc = self.nc
        xr = self.sb("xres", [128, KC, T])
        self.xres = [Tl(xr[:, c, :], name=f"xres{c}") for c in range(KC)]
        hn = self.sb("hn", [128, KC, T], F32R)
        self.hn = [Tl(hn[:, c, :], name=f"hn{c}") for c in range(KC)]
        self.NW = 3
        self.wsl = [Tl(self.sb(f"wsl{i}", [128, 4096], F32R), name=f"wsl{i}") for i in range(self.NW)]
        self.wrr = 0
        cst = self.sb("consts", [128, 3 * 128])
        self.c_all = Tl(cst, name="consts")
        self.ident = cst[:, 0:128]
        self.triu = cst[:, 128:256]
        self.negmask = cst[:, 256:384]
        self.ones_r = Tl(self.sb("ones_r", [128, 128], F32R), name="ones_r")
        self.normw = Tl(self.sb("normw", [128, 5 * KC]), name="normw")
        self.gnw = Tl(self.sb("gnw", [128, 32]), name="gnw")
        self.kqnw = Tl(self.sb("kqnw", [128, 2]), name="kqnw")
        self.convw = Tl(self.sb("convw", [128, 48 * 4]), name="convw")
        self.convb = Tl(self.sb("convb", [128, 48]), name="convb")
        self.hv = Tl(self.sb("hvec", [128, 3 * NH]), name="hvec")
        self.bfv = Tl(self.sb("bfv", [128, AH]), name="bfv")
        self.flags = Tl(self.sb("flags", [128, 2]), name="flags")
        self.totb = Tl(self.sb("totb", [128, AH]), name="totb")
        self.Aneg = Tl(self.sb("Aneg", [128, NH]), name="Aneg")
        st = self.sb("sstate", [128, NG, 512], F32R)
        self.sstate = [Tl(st[:, g, :], name=f"S{g}") for g in range(NG)]
        cr = self.sb("carry", [128, 48, 4])
        self.carry = [Tl(cr[:, c, 0:3], name=f"carry{c}") for c in range(48)]
        self.cumtot = Tl(self.sb("cumtot", [128, AH]), name="cumtot")
        msm = self.sb("msm", [128, 4, 8, NH])
        self.msm = [[Tl(msm[:, q, k, :], name=f"msm{q}_{k}") for k in range(8)] for q in range(4)]
        ub = self.sb("ubuf", [128, 2, 516])
        self.ubuf = [Tl(ub[:, i, :], name=f"ubuf{i}") for i in range(2)]
        self.urr = 0
        msr = self.sb("msr", [128, 4, 2, NH], F32R)
        self.msr = [[Tl(msr[:, q, k, :], name=f"msr{q}_{k}") for k in range(2)] for q in range(4)]
        self.triu_r = Tl(self.sb("triu_r", [128, 128], F32R), name="triu_r")
        self.ident_r = Tl(self.sb("ident_r", [128, 128], F32R), name="ident_r")
        self.negmask_r = Tl(self.sb("negmask_r", [128, 128], F32R), name="negmask_r")
        self.qcol = Tl(self.sb("qcol", [128, 1]), name="qcol")
        self.small = Tl(self.sb("small", [128, 8]), name="small")
        ksm = self.sb("ksm", [128, 8, AH])
        self.ksm = [Tl(ksm[:, k, :], name=f"ksm{k}") for k in range(8)]
        self.cumTsb = Tl(self.sb("cumTsb", [AH, T]), name="cumTsb")

        self.NPF = 11
        self.NPR = 17
        af = self.sb("arenaF", [128, self.NPF, 512])
        ar = self.sb("arenaR", [128, self.NPR, 512], F32R)
        self.af, self.ar = af, ar
        self.afb = [[Buf(f"af{p}_{q}") for q in range(4)] for p in range(self.NPF)]
        self.arb = [[Buf(f"ar{p}_{q}") for q in range(4)] for p in range(self.NPR)]
        ps = [nc.alloc_psum_tensor(f"ps{i}", [128, 512], F32).ap() for i in range(8)]
        self.ps = [Tl(ps[i], name=f"ps{i}") for i in range(8)]
        self.psrr = 0

    def pf(self, p, n=1):
        ap = self.af[:, p:p + n, :].rearrange("p a b -> p (a b)")
        return Tl(ap, [b for i in range(p, p + n) for b in self.afb[i]])

    def pfq(self, p, q, cols=128):
        return Tl(self.af[:, p, q * 128:q * 128 + cols], [self.afb[p][q]])

    def pr(self, p, n=1):
        ap = self.ar[:, p:p + n, :].rearrange("p a b -> p (a b)")
        return Tl(ap, [b for i in range(p, p + n) for b in self.arb[i]])

    def prq(self, p, q, cols=128):
        return Tl(self.ar[:, p, q * 128:q * 128 + cols], [self.arb[p][q]])

    def psum(self):
        t = self.ps[self.psrr % 6]
        self.psrr += 1
        return t

    def mm(self, out, lhsT, rhs, start, stop, reads, out_ap=None):
        oap = out.ap if out_ap is None else out_ap
        self.P.add("pe", lambda e: e.matmul(oap, lhsT=lhsT, rhs=rhs, start=start, stop=stop),
                   reads=reads, writes=[out])

    def tr(self, out, out_ap, in_ap, reads):
        kk = in_ap.shape[0]
        ir = self.ident_r
        self.P.add("pe", lambda e: e.matmul(out_ap, lhsT=in_ap, rhs=ir.ap[:kk, :kk], start=True, stop=True),
                   reads=reads + [ir], writes=[out])

    def act(self, out, in_, func, reads, bias=None, scale=None, out_ap=None, in_ap=None, extra_w=()):
        oap = out.ap if out_ap is None else out_ap
        iap = in_.ap if in_ap is None else in_ap
        kw = {}
        if bias is not None:
            kw["bias"] = bias
        if scale is not None:
            kw["scale"] = scale
        self.P.add("act", lambda e: e.activation(out=oap, in_=iap, func=func, **kw),
                   reads=reads, writes=[out] + list(extra_w))

    def dve(self, fn, reads, writes):
        self.P.add("dve", fn, reads=reads, writes=writes)

    def pl(self, name, reads, writes, **kw):
        eng = "pool" if self.cfg.get("pool_ops", False) else "dve"
        self.P.add(eng, lambda e: getattr(e, name)(**kw), reads=reads, writes=writes)

    def dv(self, name, reads, writes, **kw):
        self.P.add("dve", lambda e: getattr(e, name)(**kw), reads=reads, writes=writes)

    def dma(self, eng, out_ap, in_ap, reads, writes, stream, nonc=False, accum=False):
        if nonc:
            self.P.add(eng, lambda e: e.dma_start(out=out_ap, in_=in_ap, allow_slow_non_contiguous=True),
                       reads=reads, writes=writes, stream=stream, accum=accum)
        else:
            self.P.add(eng, lambda e: e.dma_start(out=out_ap, in_=in_ap), reads=reads, writes=writes, stream=stream,
                       accum=accum)

    def wload(self, W, r0, nk, c0, ncols):
        i = self.wrr % self.NW
        self.wrr += 1
        slot = self.wsl[i]
        src = W[r0:r0 + nk * 128, c0:c0 + ncols].rearrange("(k p) n -> p k n", p=128)
        dst = slot.ap[:, 0:nk * ncols].rearrange("p (k n) -> p k n", k=nk)
        self.dma("pool", dst, src, [], [slot], f"w{i}")
        return slot, dst

    def load_consts(self):
        P = self.P
        pairs = [(self.c_all, self.d_consts), (self.normw, self.d_normw), (self.gnw, self.d_gnw),
                 (self.kqnw, self.d_kqnw), (self.convw, self.d_convw), (self.convb, self.d_convb),
                 (self.hv, self.d_hv), (self.bfv, self.d_bf), (self.flags, self.d_flags)]
        for t, d in pairs:
            self.dma("sp", t.ap, d, [], [t], "const")
        o_r = self.ones_r
        zf = self.pf(0)
        self.dve(lambda e: e.memset(zf.ap, 1.0), [], [zf])
        self.dve(lambda e: e.tensor_copy(out=o_r.ap, in_=zf.ap[:, 0:128]), [zf], [o_r])
        self.dve(lambda e: e.memset(zf.ap, 0.0), [zf], [zf])
        hv, an = self.hv, self.Aneg
        self.act(an, hv, AF.Exp, [hv], in_ap=hv.ap[:, NH:2 * NH])
        self.dve(lambda e: e.tensor_scalar(out=an.ap, in0=an.ap, scalar1=-1.0, scalar2=None, op0=ALU.mult), [an], [an])
        for g in range(NG):
            s = self.sstate[g]
            self.dve(lambda e, s=s: e.tensor_copy(out=s.ap, in_=zf.ap), [zf], [s])
        for c in range(48):
            cr = self.carry[c]
            self.dve(lambda e, cr=cr: e.memset(cr.ap, 0.0), [], [cr])
        ct = self.cumtot
        self.dve(lambda e: e.memset(ct.ap, 0.0), [], [ct])
        ir, nr, ca = self.ident_r, self.negmask_r, self.c_all
        self.dve(lambda e: e.tensor_copy(out=ir.ap, in_=self.ident), [ca], [ir])
        self.dve(lambda e: e.tensor_copy(out=nr.ap, in_=self.negmask), [ca], [nr])
        tr_ = self.triu_r
        self.dve(lambda e: e.tensor_copy(out=tr_.ap, in_=self.triu), [ca], [tr_])
        qc, kq = self.qcol, self.kqnw
        self.dve(lambda e: e.tensor_scalar(out=qc.ap, in0=kq.ap[:, 1:2], scalar1=float(AD) ** -0.5, scalar2=None,
                                           op0=ALU.mult), [kq], [qc])

    def rstd_from_chunks(self, srcs, src_aps, nfeat, sq_tiles, rstd):
        n = len(srcs)
        ps = self.psum()
        ncol = src_aps[0].shape[-1]
        for c in range(n):
            sq = sq_tiles[c % 2]
            self.act(sq, srcs[c], AF.Square, [srcs[c]], out_ap=sq.ap[:, :ncol], in_ap=src_aps[c])
            self.mm(ps, self.ones_r.ap, sq.ap[:, :ncol], c == 0, c == n - 1, [sq, self.ones_r],
                    out_ap=ps.ap[:, :ncol])
        ra = rstd.ap[:, :ncol]
        pa = ps.ap[:, :ncol]
        self.dve(lambda e: e.tensor_scalar(out=ra, in0=pa, scalar1=1.0 / nfeat, scalar2=EPS,
                                           op0=ALU.mult, op1=ALU.add), [ps], [rstd])
        self.act(rstd, rstd, AF.Ln, [rstd], out_ap=ra, in_ap=ra)
        self.act(rstd, rstd, AF.Exp, [rstd], scale=-0.5, out_ap=ra, in_ap=ra)

    def rmsnorm_to_hn(self, widx, sq_tiles, rstd):
        self.rstd_from_chunks(self.xres, [x.ap for x in self.xres], D, sq_tiles, rstd)
        nw = self.normw
        for c in range(KC):
            x, h = self.xres[c], self.hn[c]
            col = nw.ap[:, widx * KC + c: widx * KC + c + 1]
            self.dve(lambda e, x=x, h=h, col=col: e.scalar_tensor_tensor(
                out=h.ap, in0=x.ap, scalar=col, in1=rstd.ap, op0=ALU.mult, op1=ALU.mult),
                [x, rstd, nw], [h])

    def ffn(self, layer):
        sq = [self.pr(0), self.pr(1)]
        rstd = self.pf(0)
        self.rmsnorm_to_hn(1 if layer == 0 else 4, sq, rstd)
        wgu, wdn = self.w_gu[layer], self.w_dn[layer]
        nblk = DFF // 256
        for blk in range(nblk):
            gs, gv = self.wload(wgu, 0, KC, blk * 256, 256)
            us, uv = self.wload(wgu, 0, KC, DFF + blk * 256, 256)
            pg = [self.psum(), self.psum()]
            pu = [self.psum(), self.psum()]
            for (slot, view, pss) in ((gs, gv, pg), (us, uv, pu)):
                for m in range(2):
                    for k in range(KC):
                        self.mm(pss[m], view[:, k, m * 128:(m + 1) * 128], self.hn[k].ap, k == 0, k == KC - 1,
                                [slot, self.hn[k]])
            aT = [self.pr(2 + 2 * (blk % 2)), self.pr(3 + 2 * (blk % 2))]
            for m in range(2):
                sg = self.pf(1 + (m + 2 * blk) % 4)
                self.act(sg, pg[m], AF.Silu, [pg[m]])
                a = aT[m]
                pum = pu[m]
                self.dve(lambda e, a=a, sg=sg, pum=pum: e.tensor_tensor(out=a.ap, in0=sg.ap, in1=pum.ap, op=ALU.mult),
                         [sg, pum], [a])
            ds, dv = self.wload(wdn, blk * 256, 2, 0, D)
            for c in range(KC):
                po = self.psum()
                for k in range(2):
                    self.mm(po, dv[:, k, c * 128:(c + 1) * 128], aT[k].ap, k == 0, k == 1, [ds, aT[k]])
                x = self.xres[c]
                self.dve(lambda e, x=x, po=po: e.tensor_tensor(out=x.ap, in0=x.ap, in1=po.ap, op=ALU.add),
                         [x, po], [x])


    def softplus_small(self, x, tmp1, tmp2, out, n, neg=False):
        xa, t1, t2, oa = x.ap[:, :n], tmp1.ap[:, :n], tmp2.ap[:, :n], out.ap[:, :n]
        self.act(tmp1, x, AF.Abs, [x], out_ap=t1, in_ap=xa)
        self.act(tmp1, tmp1, AF.Exp, [tmp1], scale=-1.0, out_ap=t1, in_ap=t1)
        self.act(tmp1, tmp1, AF.Ln, [tmp1], bias=1.0, out_ap=t1, in_ap=t1)
        if not neg:
            self.dve(lambda e: e.tensor_scalar_max(out=t2, in0=xa, scalar1=0.0), [x], [tmp2])
            self.dve(lambda e: e.tensor_tensor(out=oa, in0=t2, in1=t1, op=ALU.add), [tmp1, tmp2], [out])
        else:
            self.dve(lambda e: e.tensor_scalar_min(out=t2, in0=xa, scalar1=0.0), [x], [tmp2])
            self.dve(lambda e: e.tensor_tensor(out=oa, in0=t2, in1=t1, op=ALU.subtract), [tmp1, tmp2], [out])

    def conv_chunk(self, ps, cc, dest, dest_ap):
        u = self.ubuf[self.urr % 2]
        acc = self.pf(1 + self.urr % 2)
        self.urr += 1
        cr = self.carry[cc]
        cw, cb = self.convw, self.convb
        self.act(u, ps, AF.Copy, [ps], out_ap=u.ap[:, 3:515])
        self.dve(lambda e: e.tensor_copy(out=u.ap[:, 0:3], in_=cr.ap), [cr], [u])
        w = lambda k: cw.ap[:, cc * 4 + k: cc * 4 + k + 1]
        bcol = cb.ap[:, cc:cc + 1]
        self.dve(lambda e: e.tensor_scalar(out=acc.ap, in0=u.ap[:, 0:512], scalar1=w(0), scalar2=bcol,
                                           op0=ALU.mult, op1=ALU.add), [u, cw, cb], [acc])
        for k in (1, 2, 3):
            self.dve(lambda e, k=k: e.scalar_tensor_tensor(out=acc.ap, in0=u.ap[:, k:k + 512], scalar=w(k),
                                                           in1=acc.ap, op0=ALU.mult, op1=ALU.add), [u, acc, cw], [acc])
        self.dve(lambda e: e.tensor_copy(out=cr.ap, in_=u.ap[:, 512:515]), [u], [cr])
        self.act(dest, acc, AF.Silu, [acc], out_ap=dest_ap)

    def fm_proj_chunk(self, W, c0):
        raise NotImplementedError

    def mamba_tile(self, prefix=False, need_c=True):
        sq = [self.pr(0), self.pr(1)]
        rstd = self.pf(0)
        self.rmsnorm_to_hn(0, sq, rstd)
        hv = self.hv
        self.ckpt("norm")
        ds, dvw = self.wload(self.w_in, 0, KC, 10240, NH)
        for q in range(4):
            m = self.msm[q]
            ps = self.psum()
            for k in range(KC):
                self.mm(ps, self.hn[k].ap[:, q * 128:(q + 1) * 128], dvw[:, k, :], k == 0, k == KC - 1,
                        [ds, self.hn[k]], out_ap=ps.ap[:, :NH])
            x = m[7]
            self.dve(lambda e, x=x, ps=ps: e.tensor_tensor(out=x.ap, in0=ps.ap[:, :NH], in1=hv.ap[:, 0:NH], op=ALU.add),
                     [ps, hv], [x])
            self.ckpt("dt_mm")
            self.softplus_small(x, m[5], m[6], m[0], NH)
            self.ckpt("dt_sp")
            an = self.Aneg
            self.dve(lambda e, m=m: e.tensor_tensor(out=m[1].ap, in0=m[0].ap, in1=an.ap, op=ALU.mult), [m[0], an], [m[1]])
            ahi, alo = self.msr[q]
            self.dve(lambda e, m=m, ahi=ahi: e.tensor_copy(out=ahi.ap, in_=m[1].ap), [m[1]], [ahi])
            self.dve(lambda e, m=m, ahi=ahi, alo=alo: e.tensor_tensor(out=alo.ap, in0=m[1].ap, in1=ahi.ap.bitcast(F32),
                                                                     op=ALU.subtract), [m[1], ahi], [alo])
            pc, ph, pl = self.psum(), self.psum(), self.psum()
            tr_, onr = self.triu_r, self.ones_r
            self.mm(pc, tr_.ap, ahi.ap, True, False, [tr_, ahi], out_ap=pc.ap[:, :NH])
            self.mm(pc, tr_.ap, alo.ap, False, True, [tr_, alo], out_ap=pc.ap[:, :NH])
            self.mm(ph, tr_.ap, ahi.ap, True, True, [tr_, ahi], out_ap=ph.ap[:, :NH])
            self.mm(pl, onr.ap, ahi.ap, True, False, [onr, ahi], out_ap=pl.ap[:, :NH])
            self.mm(pl, onr.ap, alo.ap, False, True, [onr, alo], out_ap=pl.ap[:, :NH])
            self.dve(lambda e, m=m, pc=pc: e.tensor_copy(out=m[2].ap, in_=pc.ap[:, :NH]), [pc], [m[2]])
            self.dve(lambda e, m=m, ph=ph: e.tensor_scalar(out=m[3].ap, in0=ph.ap[:, :NH], scalar1=-1.0, scalar2=None,
                                                           op0=ALU.mult), [ph], [m[3]])
            self.act(m[4], m[2], AF.Exp, [m[2]])
            self.dve(lambda e, m=m, pl=pl: e.tensor_tensor(out=m[7].ap, in0=pl.ap[:, :NH], in1=m[2].ap, op=ALU.subtract),
                     [pl, m[2]], [m[7]])
            self.act(m[5], m[7], AF.Exp, [m[7]])
            self.dve(lambda e, m=m, pl=pl: e.tensor_copy(out=m[7].ap, in_=pl.ap[:, :NH]), [pl, m[5]], [m[7]])
            self.act(m[6], m[7], AF.Exp, [m[7]])
        if prefix:
            sufs = [Tl(self.af[:, 9, k * NH:(k + 1) * NH], self.afb[9]) for k in range(5)]
            self.dv("memset", [], [sufs[4]], ap=sufs[4].ap, constant=0.0)
            for q in (3, 2, 1, 0):
                mq = self.msm[q]
                self.dv("tensor_tensor", [mq[7], sufs[4]], [sufs[q]], out=sufs[q].ap, in0=mq[7].ap, in1=sufs[4].ap, op=ALU.add)
                self.dv("tensor_tensor", [sufs[4], mq[7]], [sufs[4]], out=sufs[4].ap, in0=sufs[4].ap, in1=mq[7].ap, op=ALU.add)
                self.dv("tensor_tensor", [sufs[q], mq[2]], [sufs[q]], out=sufs[q].ap, in0=sufs[q].ap, in1=mq[2].ap, op=ALU.subtract)
                self.act(mq[5], sufs[q], AF.Exp, [sufs[q]])
            self.act(self.msm[0][6], sufs[4], AF.Exp, [sufs[4]])
        self.ckpt("dt")
        for g in range(NG):
            xTg = [self.pr(11 + i) for i in range(4)]
            BTg, CTg = self.pr(2), self.pr(3)
            for half in range(2):
                sl, wv = self.wload(self.w_in, 0, KC, 4096 + g * 512 + half * 256, 256)
                for mch in range(2):
                    i = half * 2 + mch
                    ps = self.psum()
                    for k in range(KC):
                        self.mm(ps, wv[:, k, mch * 128:(mch + 1) * 128], self.hn[k].ap, k == 0, k == KC - 1,
                                [sl, self.hn[k]])
                    self.conv_chunk(ps, g * 4 + i, xTg[i], xTg[i].ap)
            for (which, dest, cc) in ((0, BTg, 32 + g), (1, CTg, 40 + g)):
                if which == 1 and not need_c:
                    continue
                sl, wv = self.wload(self.w_in, 0, KC, 8192 + which * 1024 + g * 128, 128)
                ps = self.psum()
                for k in range(KC):
                    self.mm(ps, wv[:, k, :], self.hn[k].ap, k == 0, k == KC - 1, [sl, self.hn[k]])
                self.conv_chunk(ps, cc, dest, dest.ap)
            self.ckpt("conv")
            sz = [self.pf(3 + q) for q in range(4)]
            if not prefix:
                zps = [self.psum() for q in range(4)]
                for half in range(2):
                    sl, wv = self.wload(self.w_in, 0, KC, g * 512 + half * 256, 256)
                    for q in range(4):
                        for k in range(KC):
                            self.mm(zps[q], self.hn[k].ap[:, q * 128:(q + 1) * 128], wv[:, k, :], k == 0, k == KC - 1,
                                    [sl, self.hn[k]], out_ap=zps[q].ap[:, half * 256:(half + 1) * 256])
                for q in range(4):
                    self.act(sz[q], zps[q], AF.Silu, [zps[q]])
            yTg = [self.pr(7 + i) for i in range(4)]
            S = self.sstate[g]
            v3 = lambda ap: ap.rearrange("p (h d) -> p h d", d=HP)

            def bcg(t, g=g):
                return t.ap[:, 8 * g:8 * g + 8].unsqueeze(2).broadcast_to([128, 8, HP])

            def front(q, g=g, xTg=xTg, BTg=BTg, CTg=CTg, S=S):
                m = self.msm[q]
                cs = slice(q * 128, (q + 1) * 128)
                ctx = {"q": q, "m": m, "cs": cs}
                psX = self.psum()
                for i in range(4):
                    self.tr(psX, psX.ap[:, i * 128:(i + 1) * 128], xTg[i].ap[:, cs], [xTg[i]])
                Xtok, xdt, xdtw = self.pf(7), self.pr(4), self.pr(5)
                if not prefix:
                    dbc = hv.ap[:, 2 * NH + 8 * g: 2 * NH + 8 * g + 8].unsqueeze(2).broadcast_to([128, 8, HP])
                    self.dv("tensor_tensor", [psX, hv], [Xtok], out=v3(Xtok.ap), in0=v3(psX.ap), in1=dbc, op=ALU.mult)
                self.dv("tensor_tensor", [psX, m[0]], [xdt], out=v3(xdt.ap), in0=v3(psX.ap), in1=bcg(m[0]), op=ALU.mult)
                self.pl("tensor_tensor", [xdt, m[5]], [xdtw], out=v3(xdtw.ap), in0=v3(xdt.ap), in1=bcg(m[5]), op=ALU.mult)
                psB = self.psum()
                self.tr(psB, psB.ap[:, 0:128], BTg.ap[:, cs], [BTg])
                Btok = self.prq(6, 0)
                self.act(Btok, psB, AF.Copy, [psB], in_ap=psB.ap[:, 0:128])
                if not prefix:
                    psC = self.psum()
                    self.mm(psC, BTg.ap[:, cs], CTg.ap[:, cs], True, True, [BTg, CTg], out_ap=psC.ap[:, 0:128])
                    CBT = self.pfq(10, 0)
                    self.dv("tensor_tensor", [psC, self.c_all], [CBT], out=CBT.ap, in0=psC.ap[:, 0:128], in1=self.triu, op=ALU.mult)
                    psY0 = self.psum()
                    self.mm(psY0, CTg.ap[:, cs], S.ap, True, True, [CTg, S])
                    y0 = self.pf(8 + q % 2)
                    self.dv("tensor_tensor", [psY0, m[4]], [y0], out=v3(y0.ap), in0=v3(psY0.ap), in1=bcg(m[4]), op=ALU.mult)
                    self.pl("tensor_tensor", [y0, Xtok], [y0], out=y0.ap, in0=y0.ap, in1=Xtok.ap, op=ALU.add)
                    ctx.update(CBT=CBT, y0=y0, xdt=xdt)
                if prefix:
                    psS = self.ps[6]
                    self.mm(psS, Btok.ap, xdtw.ap, q == 0, q == 3, [Btok, xdtw])
                    if q == 3:
                        m0 = self.msm[0]
                        self.dv("tensor_tensor", [S, m0[6]], [S], out=v3(S.ap), in0=v3(S.ap), in1=bcg(m0[6]), op=ALU.mult)
                        self.dv("tensor_tensor", [S, psS], [S], out=S.ap, in0=S.ap, in1=psS.ap, op=ALU.add)
                else:
                    psS = self.psum()
                    self.mm(psS, Btok.ap, xdtw.ap, True, True, [Btok, xdtw])
                    self.pl("tensor_tensor", [S, m[6]], [S], out=v3(S.ap), in0=v3(S.ap), in1=bcg(m[6]), op=ALU.mult)
                    self.dv("tensor_tensor", [S, psS], [S], out=S.ap, in0=S.ap, in1=psS.ap, op=ALU.add)
                return ctx

            def bc_mm(ctx, g=g):
                q = ctx["q"]
                ahi = self.msr[q][0]
                psEb = [self.psum(), self.psum()]
                ctx["psEb"] = psEb
                for hl in range(8):
                    h = 8 * g + hl
                    pe_, c4 = psEb[hl // 4], (hl % 4) * 128
                    self.mm(pe_, ahi.ap[:, h:h + 1].broadcast_to([128, 128]), self.triu_r.ap, True, True,
                            [ahi, self.triu_r], out_ap=pe_.ap[:, c4:c4 + 128])

            def rest(ctx, g=g):
                q, m, CBT, xdt, psEb = ctx["q"], ctx["m"], ctx["CBT"], ctx["xdt"], ctx["psEb"]
                psYd = self.ps[6 + q % 2]
                ctx["psYd"] = psYd
                for hl in range(8):
                    h = 8 * g + hl
                    pe_, c4 = psEb[hl // 4], (hl % 4) * 128
                    E = self.pfq(10, 1 + hl % 3)
                    self.act(E, pe_, AF.Exp, [pe_, m[3]], bias=m[3].ap[:, h:h + 1], in_ap=pe_.ap[:, c4:c4 + 128])
                    GT = self.prq(6, 1 + hl) if hl < 3 else self.prq(16, (hl - 3) % 4)
                    self.dv("scalar_tensor_tensor", [CBT, E], [GT], out=GT.ap, in0=E.ap, scalar=1.0e30, in1=CBT.ap,
                            op0=ALU.min, op1=ALU.mult)
                    self.mm(psYd, GT.ap, xdt.ap[:, hl * HP:(hl + 1) * HP], True, True, [GT, xdt],
                            out_ap=psYd.ap[:, hl * HP:(hl + 1) * HP])

            def tail_dve(ctx, g=g, sz=sz):
                q, y0, psYd = ctx["q"], ctx["y0"], ctx["psYd"]
                self.dv("tensor_tensor", [y0, psYd], [y0], out=y0.ap, in0=y0.ap, in1=psYd.ap, op=ALU.add)
                szq = sz[q]
                self.dv("tensor_tensor", [y0, szq], [y0], out=y0.ap, in0=y0.ap, in1=szq.ap, op=ALU.mult)
                sm = self.small
                self.dv("memset", [], [sm], ap=sm.ap[:, 0:1], constant=0.0)
                y0a, jka, sma = y0.ap, szq.ap, sm.ap
                self.P.add("act", lambda e, y0a=y0a, jka=jka, sma=sma: e.activation(out=jka, in_=y0a, func=AF.Square,
                                                                                  accum_out=sma[:, 0:1]),
                           reads=[y0, sm], writes=[szq, sm])
                self.dv("tensor_scalar", [sm], [sm], out=sm.ap[:, 1:2], in0=sm.ap[:, 0:1], scalar1=1.0 / 512, scalar2=EPS,
                        op0=ALU.mult, op1=ALU.add)
                self.act(sm, sm, AF.Ln, [sm], out_ap=sm.ap[:, 1:2], in_ap=sm.ap[:, 1:2])
                self.act(sm, sm, AF.Exp, [sm], scale=-0.5, out_ap=sm.ap[:, 2:3], in_ap=sm.ap[:, 1:2])
                yn = self.pr(15)
                ctx["yn"] = yn
                self.act(yn, y0, AF.Copy, [y0, sm], scale=sm.ap[:, 2:3])

            def tail_tr(ctx, g=g, yTg=yTg):
                cs, yn = ctx["cs"], ctx["yn"]
                psT = self.psum()
                for i in range(4):
                    self.tr(psT, psT.ap[:, i * 128:(i + 1) * 128], yn.ap[:, i * 128:(i + 1) * 128], [yn])
                gw = self.gnw
                for i in range(4):
                    self.act(yTg[i], psT, AF.Copy, [psT, gw], scale=gw.ap[:, g * 4 + i: g * 4 + i + 1],
                             out_ap=yTg[i].ap[:, cs], in_ap=psT.ap[:, i * 128:(i + 1) * 128])

            if prefix:
                for q in range(4):
                    front(q)
            else:
                cx = front(0)
                bc_mm(cx)
                rest(cx)
                for q in range(1, 4):
                    nx = front(q)
                    bc_mm(nx)
                    tail_dve(cx)
                    rest(nx)
                    tail_tr(cx)
                    cx = nx
                tail_dve(cx)
                tail_tr(cx)
            if not prefix:
                for half in range(2):
                    sl, wv = self.wload(self.w_out, g * 512, 4, half * 1024, 1024)
                    for cc in range(8):
                        po = self.psum()
                        for k in range(4):
                            self.mm(po, wv[:, k, cc * 128:(cc + 1) * 128], yTg[k].ap, k == 0, k == 3, [sl, yTg[k]])
                        x = self.xres[half * 8 + cc]
                        self.dve(lambda e, x=x, po=po: e.tensor_tensor(out=x.ap, in0=x.ap, in1=po.ap, op=ALU.add), [x, po], [x])


    def head_norm(self, psK, h, colap, dest, rk):
        sqk = self.pr(2 + h % 2)
        self.act(sqk, psK, AF.Square, [psK])
        psR = self.psum()
        self.mm(psR, self.ones_r.ap, sqk.ap, True, True, [self.ones_r, sqk])
        self.dv("tensor_scalar", [psR], [rk], out=rk.ap, in0=psR.ap, scalar1=1.0 / AD, scalar2=EPS, op0=ALU.mult, op1=ALU.add)
        self.act(rk, rk, AF.Ln, [rk])
        self.act(rk, rk, AF.Exp, [rk], scale=-0.5)
        self.dv("scalar_tensor_tensor", [psK, rk], [dest], out=dest.ap, in0=psK.ap, scalar=colap, in1=rk.ap,
                op0=ALU.mult, op1=ALU.mult)

    def kvf_tile(self, j):
        sq = [self.pr(0), self.pr(1)]
        rstd = self.pf(0)
        self.rmsnorm_to_hn(2, sq, rstd)
        kq = self.kqnw
        for hp in range(AH // 2):
            sl, wv = self.wload(self.w_kvf, 0, KC, hp * 256, 256)
            for mch in range(2):
                h = 2 * hp + mch
                psK = self.psum()
                for k in range(KC):
                    self.mm(psK, wv[:, k, mch * 128:(mch + 1) * 128], self.hn[k].ap, k == 0, k == KC - 1, [sl, self.hn[k]])
                kout = self.pf(3 + h % 2)
                self.head_norm(psK, h, kq.ap[:, 0:1], kout, self.pf(1 + h % 2))
                self.dma("sp", self.kt_view(j, h, False), kout.ap, [kout], [self.b_kt[j]], f"kt{h % 2}", accum=True)
        vcnt = 0
        for vb in range(8):
            sl, wv = self.wload(self.w_kvf, 0, KC, D + vb * 256, 256)
            for q in range(4):
                psV = self.psum()
                for k in range(KC):
                    self.mm(psV, self.hn[k].ap[:, q * 128:(q + 1) * 128], wv[:, k, :], k == 0, k == KC - 1,
                            [sl, self.hn[k]], out_ap=psV.ap[:, 0:256])
                p, hf = 5 + (vcnt // 2) % 2, vcnt % 2
                vout = Tl(self.af[:, p, hf * 256:(hf + 1) * 256], [self.afb[p][2 * hf], self.afb[p][2 * hf + 1]])
                self.act(vout, psV, AF.Copy, [psV], in_ap=psV.ap[:, 0:256])
                self.dma("sp", self.v_view(j, q // 2, False)[(q % 2) * 128:(q % 2 + 1) * 128, vb * 256:(vb + 1) * 256], vout.ap,
                         [vout], [self.b_v[j]], f"v{vcnt % 4}", accum=True)
                vcnt += 1
        sl, wv = self.wload(self.w_kvf, 0, KC, 2 * D, AH)
        ks = self.ksm
        ct = self.cumtot
        for q in range(4):
            psF = self.psum()
            for k in range(KC):
                self.mm(psF, self.hn[k].ap[:, q * 128:(q + 1) * 128], wv[:, k, :], k == 0, k == KC - 1,
                        [sl, self.hn[k]], out_ap=psF.ap[:, 0:AH])
            self.dv("tensor_tensor", [psF, self.bfv], [ks[0]], out=ks[0].ap, in0=psF.ap[:, 0:AH], in1=self.bfv.ap, op=ALU.add)
            self.softplus_small(ks[0], ks[1], ks[2], ks[3], AH, neg=True)
            lhi, llo = self.msr[q]
            lha, lla = lhi.ap[:, 0:AH], llo.ap[:, 0:AH]
            self.dv("tensor_copy", [ks[3]], [lhi], out=lha, in_=ks[3].ap)
            self.dv("tensor_tensor", [ks[3], lhi], [llo], out=lla, in0=ks[3].ap, in1=lha.bitcast(F32), op=ALU.subtract)
            psC, psT = self.psum(), self.psum()
            tr_, onr = self.triu_r, self.ones_r
            self.mm(psC, tr_.ap, lha, True, False, [tr_, lhi], out_ap=psC.ap[:, 0:AH])
            self.mm(psC, tr_.ap, lla, False, True, [tr_, llo], out_ap=psC.ap[:, 0:AH])
            self.mm(psT, onr.ap, lha, True, False, [onr, lhi], out_ap=psT.ap[:, 0:AH])
            self.mm(psT, onr.ap, lla, False, True, [onr, llo], out_ap=psT.ap[:, 0:AH])
            self.dv("tensor_tensor", [psC, ct], [ks[4]], out=ks[4].ap, in0=psC.ap[:, 0:AH], in1=ct.ap, op=ALU.add)
            self.dv("tensor_tensor", [psT, ct], [ct], out=ct.ap, in0=ct.ap, in1=psT.ap[:, 0:AH], op=ALU.add)
            self.dma("sp", self.cum_view(j, False)[q * 128:(q + 1) * 128, :], ks[4].ap, [ks[4]], [self.b_cum[j]],
                     "cu", accum=True)
            chi = self.prq(4, q)
            self.dv("tensor_copy", [ks[4]], [chi], out=chi.ap[:, 0:AH], in_=ks[4].ap)
            psX = self.psum()
            self.tr(psX, psX.ap[0:AH, 0:128], chi.ap[:, 0:AH], [chi])
            cts = self.cumTsb
            self.act(cts, psX, AF.Copy, [psX], out_ap=cts.ap[:, q * 128:(q + 1) * 128], in_ap=psX.ap[0:AH, 0:128])
        self.dma("sp", self.cumTd[:, j * T:(j + 1) * T], self.cumTsb.ap, [self.cumTsb], [self.b_cumT[j]], "ct")
        groups = [[2 * i, 2 * i + 1] for i in range(self.cfg.get("ncores", 8) // 2)]
        pairs = [(self.packK[j][0], self.gathK[j][0], [self.b_kt[j]]), (self.packK[j][1], self.gathK[j][1], [self.b_kt[j]]),
                 (self.packV[j][0], self.gathV[j][0], [self.b_v[j]]), (self.packV[j][1], self.gathV[j][1], [self.b_v[j]]),
                 (self.packC[j], self.gathC[j], [self.b_cum[j]])]
        for ci, (pk, gt, dep) in enumerate(pairs):
            pk2 = pk.rearrange("(r c) -> r c", r=128)
            gt2 = gt.rearrange("(r c) -> r c", r=256)
            self.P.add("pool", lambda e, pk2=pk2, gt2=gt2: e.collective_compute(
                "AllGather", ALU.bypass, replica_groups=groups, ins=[pk2], outs=[gt2]),
                reads=dep, writes=[self.b_g[j]], stream=f"cc{ci}", inc=1, accum=True)

    def make_nb(self):
        nt = self.NT
        nb = self.pf(10)
        self.nb = nb
        self.nbv = nb.ap[:, 0:4 * nt * AH].rearrange("p (k h) -> p k h", h=AH)
        self.nbpv = nb.ap[:, 256:256 + 4 * nt * AH].rearrange("p (k h) -> p k h", h=AH)
        for j in range(nt):
            self.dma("sp", self.nbv[:, 4 * j:4 * j + 4, :], self.cum_view(j, False).rearrange("(k p) h -> p k h", p=128),
                     [self.b_cum[j]], [nb], "nb", accum=True)
            self.dma("sp", self.nbpv[:, 4 * j:4 * j + 4, :], self.cum_view(j, True).rearrange("(k p) h -> p k h", p=128),
                     [self.b_g[j]], [nb], "nb", accum=True)
        tb, fl = self.totb, self.flags
        last = self.gathC[nt - 1][(T - 1) * AH: T * AH]
        self.dma("sp", tb.ap, last.partition_broadcast(128), [self.b_g[nt - 1]], [tb], "tb")
        self.dv("tensor_scalar", [tb, fl], [tb], out=tb.ap, in0=tb.ap, scalar1=fl.ap[:, 1:2], scalar2=None, op0=ALU.add)
        self.dv("tensor_scalar", [nb], [nb], out=nb.ap, in0=nb.ap, scalar1=-1.0, scalar2=None, op0=ALU.mult)
        self.dv("tensor_tensor", [nb, tb], [nb], out=self.nbpv, in0=self.nbpv,
                in1=tb.ap.unsqueeze(1).broadcast_to([128, 4 * nt, AH]), op=ALU.add)

    def attn_tile(self, j):
        sq = [self.pr(0), self.pr(1)]
        rstd = self.pf(0)
        self.rmsnorm_to_hn(3, sq, rstd)
        qc = self.qcol
        for hp in range(AH // 2):
            sl, wv = self.wload(self.w_q, 0, KC, hp * 256, 256)
            for mch in range(2):
                h = 2 * hp + mch
                psQ = self.psum()
                for k in range(KC):
                    self.mm(psQ, wv[:, k, mch * 128:(mch + 1) * 128], self.hn[k].ap, k == 0, k == KC - 1, [sl, self.hn[k]])
                qout = self.pf(3 + h % 2)
                self.head_norm(psQ, h, qc.ap[:, 0:1], qout, self.pf(1 + h % 2))
                self.dma("sp", self.qTd[h], qout.ap, [qout], [self.b_q[h]], f"q{h % 2}")
        onr, idr, nmr = self.ones_r, self.ident_r, self.negmask_r
        psO, psL = self.ps[6], self.ps[7]
        npv = 4 * self.NT
        nkb = npv + 4 * (j + 1)
        scnt = 0
        ecnt = 0
        for h in range(AH):
            qT = self.pr(2 + h % 2)
            self.dma("pool", qT.ap, self.qTd[h], [self.b_q[h]], [qT], f"lq{h % 2}")
            cpage = 4 + h % 2
            crow = Tl(self.ar[0:1, cpage, :], self.arb[cpage])
            self.dma("pool", crow.ap, self.cumTd[h:h + 1, j * T:(j + 1) * T], [self.b_cumT[j]], [crow], f"lc{h % 2}")
            items = []
            for sst in range(self.NT + j + 1):
                prev = sst < self.NT
                st = sst if prev else sst - self.NT
                for i in range(4):
                    items.append((sst, prev, st, i))
            stage_tiles = {}

            def emit_s(it):
                nonlocal scnt, ecnt
                sst, prev, st, i = it
                if sst not in stage_tiles:
                    Kst = self.pr(6 + scnt % 4)
                    Vst = self.pr(10 + scnt % 4)
                    sdep = [self.b_g[st]] if prev else [self.b_kt[st], self.b_v[st]]
                    self.dma("pool", Kst.ap, self.kt_view(st, h, prev), sdep, [Kst], f"lk{scnt % 4}")
                    vv = Vst.ap.rearrange("p (i d) -> p i d", d=AD)
                    for vh in range(2):
                        self.dma("pool", vv[:, 2 * vh:2 * vh + 2, :],
                                 self.v_view(st, vh, prev)[:, h * AD:(h + 1) * AD].rearrange("(i p) d -> p i d", p=128),
                                 sdep, [Vst], f"lv{scnt % 4}", accum=(vh == 1))
                    scnt += 1
                    stage_tiles[sst] = (Kst, Vst, vv)
                Kst, Vst, vv = stage_tiles[sst]
                diag = (not prev) and (st == j)
                nbsel = self.nbpv if prev else self.nbv
                c0 = i * 128 if diag else 0
                psS = self.psum()
                self.mm(psS, onr.ap[0:1, :], crow.ap[0:1, c0:], True, False, [onr, crow], out_ap=psS.ap[:, c0:])
                self.mm(psS, Kst.ap[:, i * 128:(i + 1) * 128], qT.ap[:, c0:], False, not diag, [Kst, qT], out_ap=psS.ap[:, c0:])
                if diag:
                    self.mm(psS, idr.ap, nmr.ap, False, True, [idr, nmr], out_ap=psS.ap[:, c0:c0 + 128])
                ET = self.pr(ecnt % 2)
                ecnt += 1
                self.act(ET, psS, AF.Exp, [psS, self.nb], bias=nbsel[:, 4 * st + i, h:h + 1], out_ap=ET.ap[:, c0:],
                         in_ap=psS.ap[:, c0:])
                return (ET, c0, Vst, vv, i, 4 * sst + i)

            esum = self.pf(6)

            def emit_pv(p):
                ET, c0, Vst, vv, i, kb = p
                self.mm(psO, vv[:, i, :], ET.ap[:, c0:], kb == 0, kb == nkb - 1, [Vst, ET], out_ap=psO.ap[:, c0:])
                if kb == 0:
                    self.dv("tensor_copy", [ET], [esum], out=esum.ap, in_=ET.ap.bitcast(F32))
                else:
                    self.dv("tensor_tensor", [ET, esum], [esum], out=esum.ap[:, c0:], in0=esum.ap[:, c0:],
                            in1=ET.ap[:, c0:].bitcast(F32), op=ALU.add)

            pend = None
            for it in items:
                cur = emit_s(it)
                if pend is not None:
                    emit_pv(pend)
                pend = cur
            emit_pv(pend)
            esr = self.pr(14)
            self.dv("tensor_copy", [esum], [esr], out=esr.ap, in_=esum.ap)
            self.mm(psL, onr.ap, esr.ap, True, True, [onr, esr])
            rl = self.pf(5)
            self.dv("reciprocal", [psL], [rl], out=rl.ap, in_=psL.ap)
            oT = self.hn[h]
            self.dv("tensor_tensor", [psO, rl], [oT], out=oT.ap, in0=psO.ap, in1=rl.ap, op=ALU.mult)
        for cb in range(8):
            sl, wv = self.wload(self.w_o, 0, KC, cb * 256, 256)
            for mch in range(2):
                po = self.psum()
                for k in range(KC):
                    self.mm(po, wv[:, k, mch * 128:(mch + 1) * 128], self.hn[k].ap, k == 0, k == KC - 1, [sl, self.hn[k]])
                x = self.xres[2 * cb + mch]
                self.dv("tensor_tensor", [x, po], [x], out=x.ap, in0=x.ap, in1=po.ap, op=ALU.add)

    def load_tile(self, src, j, dep=None):
        for c in range(KC):
            x = self.xres[c]
            self.dma("sp", x.ap, src[c, :, j * T:(j + 1) * T], [dep] if dep else [], [x], f"x{c}")

    def store_tile(self, dst, j, tag=None):
        for c in range(KC):
            x = self.xres[c]
            self.dma("sp", dst[c, :, j * T:(j + 1) * T], x.ap, [x], [tag] if tag else [], f"o{c}", accum=True)


def _build(S, cfg):
    B = Builder(S, cfg)
    B.load_consts()
    mode = cfg.get("mode", "full")
    if mode == "full":
        nt = cfg.get("ntiles", B.NT)
        for j in range(nt if cfg.get("prefix_pass", True) else 0):
            B.load_tile(B.xpT, j)
            B.mamba_tile(prefix=True, need_c=(j == nt - 1))
        fl = B.flags
        for g in range(NG):
            st_ = B.sstate[g]
            B.dv("tensor_scalar", [st_, fl], [st_], out=st_.ap, in0=st_.ap, scalar1=fl.ap[:, 0:1], scalar2=None, op0=ALU.mult)
        for c in range(48):
            cr = B.carry[c]
            B.dv("tensor_scalar", [cr, fl], [cr], out=cr.ap, in0=cr.ap, scalar1=fl.ap[:, 0:1], scalar2=None, op0=ALU.mult)
        for j in range(nt):
            B.load_tile(B.xT, j)
            B.mamba_tile()
            B.ffn(0)
            B.kvf_tile(j)
            B.store_tile(B.h1T, j, tag=B.b_h1[j])
        B.make_nb()
        for j in range(nt if cfg.get("sweep2", True) else 0):
            B.load_tile(B.h1T, j, dep=B.b_h1[j])
            if cfg.get("attn", True):
                B.attn_tile(j)
            if cfg.get("ffn1", True):
                B.ffn(1)
            B.store_tile(B.outT, j)
    if mode == "mamba_only":
        for j in range(cfg.get("ntiles", 1)):
            B.load_tile(B.xT, j)
            try:
                B.mamba_tile(prefix=cfg.get("prefix", False))
            except StopBuild:
                pass
            B.store_tile(B.outT, j)
    if mode == "ffn_only":
        for j in range(cfg.get("ntiles", 1)):
            B.load_tile(B.xT, j)
            B.ffn(0)
            B.store_tile(B.outT, j)
    B.P.emit()
    return B


def _layout_inputs(inp):
    f = np.float32
    col = lambda v: np.ascontiguousarray(np.asarray(v, f).reshape(-1, 128).T)
    rep = lambda v: np.ascontiguousarray(np.broadcast_to(np.asarray(v, f).reshape(1, -1), (128, np.asarray(v).size)))
    normw = np.concatenate([col(inp["a_norm_w"][0]), col(inp["ffn_norm_w"][0]), col(inp["kv_norm_w"]),
                            col(inp["b_norm_w"][0]), col(inp["ffn_norm_w"][1])], axis=1)
    cw = np.asarray(inp["a_conv_w"][0], f)
    convw = np.ascontiguousarray(cw.reshape(4, 48, 128).transpose(2, 1, 0).reshape(128, 48 * 4))
    ident = np.eye(128, dtype=f)
    triu = np.triu(np.ones((128, 128), f))
    negmask = (np.tril(np.ones((128, 128), f), -1) * NEG).astype(f)
    shared = {
        "w_in": np.ascontiguousarray(inp["a_in_proj"][0], f), "w_out": np.ascontiguousarray(inp["a_out_proj"][0], f),
        "w_gu0": np.ascontiguousarray(inp["w_gate_up"][0], f), "w_gu1": np.ascontiguousarray(inp["w_gate_up"][1], f),
        "w_dn0": np.ascontiguousarray(inp["w_down"][0], f), "w_dn1": np.ascontiguousarray(inp["w_down"][1], f),
        "w_kvf": np.ascontiguousarray(inp["w_kvf"], f), "w_q": np.ascontiguousarray(inp["w_q"][0], f),
        "w_o": np.ascontiguousarray(inp["w_o"][0], f),
        "normw": np.ascontiguousarray(normw), "gnw": col(inp["a_gnorm_w"][0]),
        "kqnw": np.ascontiguousarray(np.stack([np.asarray(inp["k_norm_w"], f), np.asarray(inp["q_norm_w"][0], f)], axis=1)),
        "convw": convw, "convb": col(inp["a_conv_b"][0]),
        "hvec": np.concatenate([rep(inp["a_dt_bias"][0]), rep(inp["a_A_log"][0]), rep(inp["a_D"][0])], axis=1),
        "bfv": rep(inp["b_f"]),
        "consts": np.concatenate([ident, triu, negmask], axis=1),
    }
    return shared


def _xT(xb):
    S = xb.shape[0]
    return np.ascontiguousarray(np.asarray(xb, np.float32).T.reshape(KC, 128, S))


def kernel(**inputs):
    x = np.asarray(inputs["x"], np.float32)
    nb, S, _ = x.shape
    H = S // 2
    shared = _layout_inputs(inputs)
    B = _build(H, {"mode": "full"})
    in_maps = []
    for b in range(nb):
        first = _xT(x[b, :H])
        for half in range(2):
            m = dict(shared)
            m["xT"] = first if half == 0 else _xT(x[b, H:])
            m["xpT"] = first
            fl = np.zeros((128, 2), np.float32)
            fl[:, 0] = float(half)
            fl[:, 1] = 0.0 if half == 1 else NEG
            m["flags"] = fl
            in_maps.append(m)
    res = run_bass_kernel_spmd(B.nc, in_maps, core_ids=list(range(2 * nb)))
    out = np.empty((nb, S, D), np.float32)
    for b in range(nb):
        for half in range(2):
            o = res.results[2 * b + half]["outT"]
            out[b, half * H:(half + 1) * H] = o.reshape(D, H).T
    return out
```

```python
import contextlib
import numpy as np
import concourse.bass as bass
import concourse.mybir as mybir
from concourse.bass_utils import run_bass_kernel_spmd

F32 = mybir.dt.float32
F32R = mybir.dt.float32r
AF = mybir.ActivationFunctionType
ALU = mybir.AluOpType
AX = mybir.AxisListType

D = 2048
DI = 4096
NH = 64
HP = 64
NG = 8
NS = 128
DXBC = 6144
DINP = 10304
DFF = 5632
AH = 16
AD = 128
T = 512
KC = D // 128
EPS = 1e-6
NEG = -30000.0


class Buf:
    __slots__ = ("name", "w", "r")

    def __init__(self, name=""):
        self.name = name
        self.w = []
        self.r = {}


class Tl:
    __slots__ = ("ap", "bufs")

    def __init__(self, ap, bufs=None, name=""):
        self.ap = ap
        self.bufs = bufs if bufs is not None else [Buf(name)]


class Op:
    __slots__ = ("eng", "fn", "deps", "idx", "sig", "sem", "semval", "dma", "stream", "inc")


class Prog:
    ENGS = ("pe", "act", "dve", "pool", "sp")

    def __init__(self, nc):
        self.nc = nc
        self.ops = []
        self.streams = {}

    def add(self, eng, fn, reads=(), writes=(), stream=None, accum=False, inc=16):
        op = Op()
        op.inc = inc
        op.eng, op.fn, op.idx = eng, fn, len(self.ops)
        op.dma = stream is not None
        op.stream = stream
        op.sig = op.dma
        deps = set()
        rb = [b for t in reads for b in t.bufs]
        wb = [b for t in writes for b in t.bufs]
        for b in rb:
            deps.update(b.w)
        for b in wb:
            deps.update(b.w)
            deps.update(b.r.values())
        op.deps = deps
        rkey = (eng, stream) if stream is not None else eng
        for b in rb:
            b.r[rkey] = op.idx
        for b in wb:
            if accum:
                b.w.append(op.idx)
            else:
                b.w = [op.idx]
            b.r = {}
        if op.dma:
            st = self.streams.setdefault(stream, {"count": 0})
            st["count"] += inc
            op.semval = st["count"]
        self.ops.append(op)
        return op

    def emit(self):
        nc = self.nc
        ops = self.ops
        for op in ops:
            for d in op.deps:
                dop = ops[d]
                if dop.dma:
                    continue
                if dop.eng == op.eng and op.eng == "pe" and not op.dma:
                    continue
                dop.sig = True
        with contextlib.ExitStack() as es:
            esem = {e: es.enter_context(nc.semaphore("s_" + e)) for e in self.ENGS}
            ssem = {s: es.enter_context(nc.semaphore("d_" + s)) for s in self.streams}
            cnt = {e: 0 for e in self.ENGS}
            for op in ops:
                if op.dma:
                    op.sem = ssem[op.stream]
                    if op.stream.startswith("const"):
                        op.semval = self.streams[op.stream]["count"]
                else:
                    op.sem = esem[op.eng]
                    if op.sig:
                        cnt[op.eng] += 1
                        op.semval = cnt[op.eng]
            block = es.enter_context(nc.Block())
            per = {e: [o for o in ops if o.eng == e] for e in self.ENGS}
            final_waits = [(ssem[s], st["count"]) for s, st in self.streams.items()]

            def run(e, ename):
                seen = {}
                for op in per[ename]:
                    for d in sorted(op.deps):
                        dop = ops[d]
                        if (not dop.dma) and dop.eng == ename and ename == "pe" and not op.dma:
                            continue
                        key = id(dop.sem)
                        if seen.get(key, 0) >= dop.semval:
                            continue
                        e.wait_ge(dop.sem, dop.semval)
                        seen[key] = dop.semval
                    ins = op.fn(e)
                    if op.sig:
                        ins.then_inc(op.sem, op.inc if op.dma else 1)
                if ename == "sp":
                    for s, v in final_waits:
                        e.wait_ge(s, v)

            @block.tensor
            def _(e):
                run(e, "pe")

            @block.scalar
            def _(e):
                run(e, "act")

            @block.vector
            def _(e):
                run(e, "dve")

            @block.gpsimd
            def _(e):
                run(e, "pool")

            @block.sync
            def _(e):
                run(e, "sp")


class StopBuild(Exception):
    pass


class Builder:
    def ckpt(self, name):
        if self.cfg.get("stop") == name:
            raise StopBuild(name)

    def __init__(self, S, cfg):
        self.S = S
        self.cfg = cfg
        self.NT = S // T
        nc = bass.Bass("TRN2", target_bir_lowering=False)
        self.nc = nc
        self.P = Prog(nc)
        self.dram_in = {}
        self._mk_dram()
        self._mk_sbuf()

    def din(self, name, shape):
        t = self.nc.dram_tensor(name, list(shape), F32, kind="ExternalInput").ap()
        self.dram_in[name] = t
        return t

    def _mk_dram(self):
        S = self.S
        nc = self.nc
        self.xT = self.din("xT", [KC, 128, S])
        self.w_in = self.din("w_in", [D, DINP])
        self.w_out = self.din("w_out", [DI, D])
        self.w_gu = [self.din("w_gu0", [D, 2 * DFF]), self.din("w_gu1", [D, 2 * DFF])]
        self.w_dn = [self.din("w_dn0", [DFF, D]), self.din("w_dn1", [DFF, D])]
        self.w_kvf = self.din("w_kvf", [D, 2 * D + AH])
        self.w_q = self.din("w_q", [D, D])
        self.w_o = self.din("w_o", [D, D])
        self.d_normw = self.din("normw", [128, 5 * KC])
        self.d_gnw = self.din("gnw", [128, DI // 128])
        self.d_kqnw = self.din("kqnw", [128, 2])
        self.d_convw = self.din("convw", [128, 48 * 4])
        self.d_convb = self.din("convb", [128, 48])
        self.d_hv = self.din("hvec", [128, 3 * NH])
        self.d_bf = self.din("bfv", [128, AH])
        self.d_consts = self.din("consts", [128, 3 * 128])
        self.outT = nc.dram_tensor("outT", [KC, 128, S], F32, kind="ExternalOutput").ap()
        self.xpT = self.din("xpT", [KC, 128, S])
        self.d_flags = self.din("flags", [128, 2])
        self.h1T = nc.dram_tensor("h1T", [KC, 128, S], F32).ap()
        self.cumTd = nc.dram_tensor("cumTd", [AH, S], F32).ap()
        self.PKh = (AH // 2) * 128 * T
        self.PVh = (T // 2) * D
        self.PC = T * AH
        mk = lambda n, sz: nc.dram_tensor(n, [sz], F32).ap()
        self.packK = [[mk(f"pK{j}_{i}", self.PKh) for i in range(2)] for j in range(self.NT)]
        self.gathK = [[mk(f"gK{j}_{i}", 2 * self.PKh) for i in range(2)] for j in range(self.NT)]
        self.packV = [[mk(f"pV{j}_{i}", self.PVh) for i in range(2)] for j in range(self.NT)]
        self.gathV = [[mk(f"gV{j}_{i}", 2 * self.PVh) for i in range(2)] for j in range(self.NT)]
        self.packC = [mk(f"pC{j}", self.PC) for j in range(self.NT)]
        self.gathC = [mk(f"gC{j}", 2 * self.PC) for j in range(self.NT)]
        self.b_g = [Tl(None, name=f"g_{j}") for j in range(self.NT)]
        self.qTd = nc.dram_tensor("qTd", [AH, 128, T], F32).ap()
        self.b_q = [Tl(None, name=f"q_{h}") for h in range(AH)]
        nt = self.NT
        self.b_h1 = [Tl(None, name=f"h1_{j}") for j in range(nt)]
        self.b_kt = [Tl(None, name=f"kt_{j}") for j in range(nt)]
        self.b_v = [Tl(None, name=f"v_{j}") for j in range(nt)]
        self.b_cum = [Tl(None, name=f"cum_{j}") for j in range(nt)]
        self.b_cumT = [Tl(None, name=f"cumT_{j}") for j in range(nt)]

    def kt_view(self, j, h, prev):
        flat = (self.gathK if prev else self.packK)[j][h // 8]
        return flat[0:self.PKh].rearrange("(h d t) -> h d t", h=AH // 2, d=128)[h % 8]

    def v_view(self, j, vh, prev):
        flat = (self.gathV if prev else self.packV)[j][vh]
        return flat[0:self.PVh].rearrange("(t c) -> t c", c=D)

    def cum_view(self, j, prev):
        flat = (self.gathC if prev else self.packC)[j]
        return flat[0:self.PC].rearrange("(t h) -> t h", h=AH)

    def sb(self, name, shape, dt=F32):
        return self.nc.alloc_sbuf_tensor("sb_" + name, list(shape), dt).ap()

    def _mk_sbuf(self):
        nc = self.nc
        xr = self.sb("xres", [128, KC, T])
        self.xres = [Tl(xr[:, c, :], name=f"xres{c}") for c in range(KC)]
        hn = self.sb("hn", [128, KC, T], F32R)
        self.hn = [Tl(hn[:, c, :], name=f"hn{c}") for c in range(KC)]
        self.NW = 3
        self.wsl = [Tl(self.sb(f"wsl{i}", [128, 4096], F32R), name=f"wsl{i}") for i in range(self.NW)]
        self.wrr = 0
        cst = self.sb("consts", [128, 3 * 128])
        self.c_all = Tl(cst, name="consts")
        self.ident = cst[:, 0:128]
        self.triu = cst[:, 128:256]
        self.negmask = cst[:, 256:384]
        self.ones_r = Tl(self.sb("ones_r", [128, 128], F32R), name="ones_r")
        self.normw = Tl(self.sb("normw", [128, 5 * KC]), name="normw")
        self.gnw = Tl(self.sb("gnw", [128, 32]), name="gnw")
        self.kqnw = Tl(self.sb("kqnw", [128, 2]), name="kqnw")
        self.convw = Tl(self.sb("convw", [128, 48 * 4]), name="convw")
        self.convb = Tl(self.sb("convb", [128, 48]), name="convb")
        self.hv = Tl(self.sb("hvec", [128, 3 * NH]), name="hvec")
        self.bfv = Tl(self.sb("bfv", [128, AH]), name="bfv")
        self.flags = Tl(self.sb("flags", [128, 2]), name="flags")
        self.totb = Tl(self.sb("totb", [128, AH]), name="totb")
        self.Aneg = Tl(self.sb("Aneg", [128, NH]), name="Aneg")
        st = self.sb("sstate", [128, NG, 512], F32R)
        self.sstate = [Tl(st[:, g, :], name=f"S{g}") for g in range(NG)]
        cr = self.sb("carry", [128, 48, 4])
        self.carry = [Tl(cr[:, c, 0:3], name=f"carry{c}") for c in range(48)]
        self.cumtot = Tl(self.sb("cumtot", [128, AH]), name="cumtot")
        msm = self.sb("msm", [128, 4, 8, NH])
        self.msm = [[Tl(msm[:, q, k, :], name=f"msm{q}_{k}") for k in range(8)] for q in range(4)]
        ub = self.sb("ubuf", [128, 2, 516])
        self.ubuf = [Tl(ub[:, i, :], name=f"ubuf{i}") for i in range(2)]
        self.urr = 0
        msr = self.sb("msr", [128, 4, 2, NH], F32R)
        self.msr = [[Tl(msr[:, q, k, :], name=f"msr{q}_{k}") for k in range(2)] for q in range(4)]
        self.triu_r = Tl(self.sb("triu_r", [128, 128], F32R), name="triu_r")
        self.ident_r = Tl(self.sb("ident_r", [128, 128], F32R), name="ident_r")
        self.negmask_r = Tl(self.sb("negmask_r", [128, 128], F32R), name="negmask_r")
        self.qcol = Tl(self.sb("qcol", [128, 1]), name="qcol")
        self.small = Tl(self.sb("small", [128, 8]), name="small")
        ksm = self.sb("ksm", [128, 8, AH])
        self.ksm = [Tl(ksm[:, k, :], name=f"ksm{k}") for k in range(8)]
        self.cumTsb = Tl(self.sb("cumTsb", [AH, T]), name="cumTsb")

        self.NPF = 11
        self.NPR = 17
        af = self.sb("arenaF", [128, self.NPF, 512])
        ar = self.sb("arenaR", [128, self.NPR, 512], F32R)
        self.af, self.ar = af, ar
        self.afb = [[Buf(f"af{p}_{q}") for q in range(4)] for p in range(self.NPF)]
        self.arb = [[Buf(f"ar{p}_{q}") for q in range(4)] for p in range(self.NPR)]
        ps = [nc.alloc_psum_tensor(f"ps{i}", [128, 512], F32).ap() for i in range(8)]
        self.ps = [Tl(ps[i], name=f"ps{i}") for i in range(8)]
        self.psrr = 0

    def pf(self, p, n=1):
        ap = self.af[:, p:p + n, :].rearrange("p a b -> p (a b)")
        return Tl(ap, [b for i in range(p, p + n) for b in self.afb[i]])

    def pfq(self, p, q, cols=128):
        return Tl(self.af[:, p, q * 128:q * 128 + cols], [self.afb[p][q]])

    def pr(self, p, n=1):
        ap = self.ar[:, p:p + n, :].rearrange("p a b -> p (a b)")
        return Tl(ap, [b for i in range(p, p + n) for b in self.arb[i]])

    def prq(self, p, q, cols=128):
        return Tl(self.ar[:, p, q * 128:q * 128 + cols], [self.arb[p][q]])

    def psum(self):
        t = self.ps[self.psrr % 6]
        self.psrr += 1
        return t

    def mm(self, out, lhsT, rhs, start, stop, reads, out_ap=None):
        oap = out.ap if out_ap is None else out_ap
        self.P.add("pe", lambda e: e.matmul(oap, lhsT=lhsT, rhs=rhs, start=start, stop=stop),
                   reads=reads, writes=[out])

    def tr(self, out, out_ap, in_ap, reads):
        kk = in_ap.shape[0]
        ir = self.ident_r
        self.P.add("pe", lambda e: e.matmul(out_ap, lhsT=in_ap, rhs=ir.ap[:kk, :kk], start=True, stop=True),
                   reads=reads + [ir], writes=[out])

    def act(self, out, in_, func, reads, bias=None, scale=None, out_ap=None, in_ap=None, extra_w=()):
        oap = out.ap if out_ap is None else out_ap
        iap = in_.ap if in_ap is None else in_ap
        kw = {}
        if bias is not None:
            kw["bias"] = bias
        if scale is not None:
            kw["scale"] = scale
        self.P.add("act", lambda e: e.activation(out=oap, in_=iap, func=func, **kw),
                   reads=reads, writes=[out] + list(extra_w))

    def dve(self, fn, reads, writes):
        self.P.add("dve", fn, reads=reads, writes=writes)

    def pl(self, name, reads, writes, **kw):
        eng = "pool" if self.cfg.get("pool_ops", False) else "dve"
        self.P.add(eng, lambda e: getattr(e, name)(**kw), reads=reads, writes=writes)

    def dv(self, name, reads, writes, **kw):
        self.P.add("dve", lambda e: getattr(e, name)(**kw), reads=reads, writes=writes)

    def dma(self, eng, out_ap, in_ap, reads, writes, stream, nonc=False, accum=False):
        if nonc:
            self.P.add(eng, lambda e: e.dma_start(out=out_ap, in_=in_ap, allow_slow_non_contiguous=True),
                       reads=reads, writes=writes, stream=stream, accum=accum)
        else:
            self.P.add(eng, lambda e: e.dma_start(out=out_ap, in_=in_ap), reads=reads, writes=writes, stream=stream,
                       accum=accum)

    def wload(self, W, r0, nk, c0, ncols):
        i = self.wrr % self.NW
        self.wrr += 1
        slot = self.wsl[i]
        src = W[r0:r0 + nk * 128, c0:c0 + ncols].rearrange("(k p) n -> p k n", p=128)
        dst = slot.ap[:, 0:nk * ncols].rearrange("p (k n) -> p k n", k=nk)
        self.dma("pool", dst, src, [], [slot], f"w{i}")
        return slot, dst

    def load_consts(self):
        P = self.P
        pairs = [(self.c_all, self.d_consts), (self.normw, self.d_normw), (self.gnw, self.d_gnw),
                 (self.kqnw, self.d_kqnw), (self.convw, self.d_convw), (self.convb, self.d_convb),
                 (self.hv, self.d_hv), (self.bfv, self.d_bf), (self.flags, self.d_flags)]
        for t, d in pairs:
            self.dma("sp", t.ap, d, [], [t], "const")
        o_r = self.ones_r
        zf = self.pf(0)
        self.dve(lambda e: e.memset(zf.ap, 1.0), [], [zf])
        self.dve(lambda e: e.tensor_copy(out=o_r.ap, in_=zf.ap[:, 0:128]), [zf], [o_r])
        self.dve(lambda e: e.memset(zf.ap, 0.0), [zf], [zf])
        hv, an = self.hv, self.Aneg
        self.act(an, hv, AF.Exp, [hv], in_ap=hv.ap[:, NH:2 * NH])
        self.dve(lambda e: e.tensor_scalar(out=an.ap, in0=an.ap, scalar1=-1.0, scalar2=None, op0=ALU.mult), [an], [an])
        for g in range(NG):
            s = self.sstate[g]
            self.dve(lambda e, s=s: e.tensor_copy(out=s.ap, in_=zf.ap), [zf], [s])
        for c in range(48):
            cr = self.carry[c]
            self.dve(lambda e, cr=cr: e.memset(cr.ap, 0.0), [], [cr])
        ct = self.cumtot
        self.dve(lambda e: e.memset(ct.ap, 0.0), [], [ct])
        ir, nr, ca = self.ident_r, self.negmask_r, self.c_all
        self.dve(lambda e: e.tensor_copy(out=ir.ap, in_=self.ident), [ca], [ir])
        self.dve(lambda e: e.tensor_copy(out=nr.ap, in_=self.negmask), [ca], [nr])
        tr_ = self.triu_r
        self.dve(lambda e: e.tensor_copy(out=tr_.ap, in_=self.triu), [ca], [tr_])
        qc, kq = self.qcol, self.kqnw
        self.dve(lambda e: e.tensor_scalar(out=qc.ap, in0=kq.ap[:, 1:2], scalar1=float(AD) ** -0.5, scalar2=None,
                                           op0=ALU.mult), [kq], [qc])

    def rstd_from_chunks(self, srcs, src_aps, nfeat, sq_tiles, rstd):
        n = len(srcs)
        ps = self.psum()
        ncol = src_aps[0].shape[-1]
        for c in range(n):
            sq = sq_tiles[c % 2]
            self.act(sq, srcs[c], AF.Square, [srcs[c]], out_ap=sq.ap[:, :ncol], in_ap=src_aps[c])
            self.mm(ps, self.ones_r.ap, sq.ap[:, :ncol], c == 0, c == n - 1, [sq, self.ones_r],
                    out_ap=ps.ap[:, :ncol])
        ra = rstd.ap[:, :ncol]
        pa = ps.ap[:, :ncol]
        self.dve(lambda e: e.tensor_scalar(out=ra, in0=pa, scalar1=1.0 / nfeat, scalar2=EPS,
                                           op0=ALU.mult, op1=ALU.add), [ps], [rstd])
        self.act(rstd, rstd, AF.Ln, [rstd], out_ap=ra, in_ap=ra)
        self.act(rstd, rstd, AF.Exp, [rstd], scale=-0.5, out_ap=ra, in_ap=ra)

    def rmsnorm_to_hn(self, widx, sq_tiles, rstd):
        self.rstd_from_chunks(self.xres, [x.ap for x in self.xres], D, sq_tiles, rstd)
        nw = self.normw
        for c in range(KC):
            x, h = self.xres[c], self.hn[c]
            col = nw.ap[:, widx * KC + c: widx * KC + c + 1]
            self.dve(lambda e, x=x, h=h, col=col: e.scalar_tensor_tensor(
                out=h.ap, in0=x.ap, scalar=col, in1=rstd.ap, op0=ALU.mult, op1=ALU.mult),
                [x, rstd, nw], [h])

    def ffn(self, layer):
        sq = [self.pr(0), self.pr(1)]
        rstd = self.pf(0)
        self.rmsnorm_to_hn(1 if layer == 0 else 4, sq, rstd)
        wgu, wdn = self.w_gu[layer], self.w_dn[layer]
        nblk = DFF // 256
        for blk in range(nblk):
            gs, gv = self.wload(wgu, 0, KC, blk * 256, 256)
            us, uv = self.wload(wgu, 0, KC, DFF + blk * 256, 256)
            pg = [self.psum(), self.psum()]
            pu = [self.psum(), self.psum()]
            for (slot, view, pss) in ((gs, gv, pg), (us, uv, pu)):
                for m in range(2):
                    for k in range(KC):
                        self.mm(pss[m], view[:, k, m * 128:(m + 1) * 128], self.hn[k].ap, k == 0, k == KC - 1,
                                [slot, self.hn[k]])
            aT = [self.pr(2 + 2 * (blk % 2)), self.pr(3 + 2 * (blk % 2))]
            for m in range(2):
                sg = self.pf(1 + (m + 2 * blk) % 4)
                self.act(sg, pg[m], AF.Silu, [pg[m]])
                a = aT[m]
                pum = pu[m]
                self.dve(lambda e, a=a, sg=sg, pum=pum: e.tensor_tensor(out=a.ap, in0=sg.ap, in1=pum.ap, op=ALU.mult),
                         [sg, pum], [a])
            ds, dv = self.wload(wdn, blk * 256, 2, 0, D)
            for c in range(KC):
                po = self.psum()
                for k in range(2):
                    self.mm(po, dv[:, k, c * 128:(c + 1) * 128], aT[k].ap, k == 0, k == 1, [ds, aT[k]])
                x = self.xres[c]
                self.dve(lambda e, x=x, po=po: e.tensor_tensor(out=x.ap, in0=x.ap, in1=po.ap, op=ALU.add),
                         [x, po], [x])


    def softplus_small(self, x, tmp1, tmp2, out, n, neg=False):
        xa, t1, t2, oa = x.ap[:, :n], tmp1.ap[:, :n], tmp2.ap[:, :n], out.ap[:, :n]
        self.act(tmp1, x, AF.Abs, [x], out_ap=t1, in_ap=xa)
        self.act(tmp1, tmp1, AF.Exp, [tmp1], scale=-1.0, out_ap=t1, in_ap=t1)
        self.act(tmp1, tmp1, AF.Ln, [tmp1], bias=1.0, out_ap=t1, in_ap=t1)
        if not neg:
            self.dve(lambda e: e.tensor_scalar_max(out=t2, in0=xa, scalar1=0.0), [x], [tmp2])
            self.dve(lambda e: e.tensor_tensor(out=oa, in0=t2, in1=t1, op=ALU.add), [tmp1, tmp2], [out])
        else:
            self.dve(lambda e: e.tensor_scalar_min(out=t2, in0=xa, scalar1=0.0), [x], [tmp2])
            self.dve(lambda e: e.tensor_tensor(out=oa, in0=t2, in1=t1, op=ALU.subtract), [tmp1, tmp2], [out])

    def conv_chunk(self, ps, cc, dest, dest_ap):
        u = self.ubuf[self.urr % 2]
        acc = self.pf(1 + self.urr % 2)
        self.urr += 1
        cr = self.carry[cc]
        cw, cb = self.convw, self.convb
        self.act(u, ps, AF.Copy, [ps], out_ap=u.ap[:, 3:515])
        self.dve(lambda e: e.tensor_copy(out=u.ap[:, 0:3], in_=cr.ap), [cr], [u])
        w = lambda k: cw.ap[:, cc * 4 + k: cc * 4 + k + 1]
        bcol = cb.ap[:, cc:cc + 1]
        self.dve(lambda e: e.tensor_scalar(out=acc.ap, in0=u.ap[:, 0:512], scalar1=w(0), scalar2=bcol,
                                           op0=ALU.mult, op1=ALU.add), [u, cw, cb], [acc])
        for k in (1, 2, 3):
            self.dve(lambda e, k=k: e.scalar_tensor_tensor(out=acc.ap, in0=u.ap[:, k:k + 512], scalar=w(k),
                                                           in1=acc.ap, op0=ALU.mult, op1=ALU.add), [u, acc, cw], [acc])
        self.dve(lambda e: e.tensor_copy(out=cr.ap, in_=u.ap[:, 512:515]), [u], [cr])
        self.act(dest, acc, AF.Silu, [acc], out_ap=dest_ap)

    def fm_proj_chunk(self, W, c0):
        raise NotImplementedError

    def mamba_tile(self, prefix=False, need_c=True):
        sq = [self.pr(0), self.pr(1)]
        rstd = self.pf(0)
        self.rmsnorm_to_hn(0, sq, rstd)
        hv = self.hv
        self.ckpt("norm")
        ds, dvw = self.wload(self.w_in, 0, KC, 10240, NH)
        for q in range(4):
            m = self.msm[q]
            ps = self.psum()
            for k in range(KC):
                self.mm(ps, self.hn[k].ap[:, q * 128:(q + 1) * 128], dvw[:, k, :], k == 0, k == KC - 1,
                        [ds, self.hn[k]], out_ap=ps.ap[:, :NH])
            x = m[7]
            self.dve(lambda e, x=x, ps=ps: e.tensor_tensor(out=x.ap, in0=ps.ap[:, :NH], in1=hv.ap[:, 0:NH], op=ALU.add),
                     [ps, hv], [x])
            self.ckpt("dt_mm")
            self.softplus_small(x, m[5], m[6], m[0], NH)
            self.ckpt("dt_sp")
            an = self.Aneg
            self.dve(lambda e, m=m: e.tensor_tensor(out=m[1].ap, in0=m[0].ap, in1=an.ap, op=ALU.mult), [m[0], an], [m[1]])
            ahi, alo = self.msr[q]
            self.dve(lambda e, m=m, ahi=ahi: e.tensor_copy(out=ahi.ap, in_=m[1].ap), [m[1]], [ahi])
            self.dve(lambda e, m=m, ahi=ahi, alo=alo: e.tensor_tensor(out=alo.ap, in0=m[1].ap, in1=ahi.ap.bitcast(F32),
                                                                     op=ALU.subtract), [m[1], ahi], [alo])
            pc, ph, pl = self.psum(), self.psum(), self.psum()
            tr_, onr = self.triu_r, self.ones_r
            self.mm(pc, tr_.ap, ahi.ap, True, False, [tr_, ahi], out_ap=pc.ap[:, :NH])
            self.mm(pc, tr_.ap, alo.ap, False, True, [tr_, alo], out_ap=pc.ap[:, :NH])
            self.mm(ph, tr_.ap, ahi.ap, True, True, [tr_, ahi], out_ap=ph.ap[:, :NH])
            self.mm(pl, onr.ap, ahi.ap, True, False, [onr, ahi], out_ap=pl.ap[:, :NH])
            self.mm(pl, onr.ap, alo.ap, False, True, [onr, alo], out_ap=pl.ap[:, :NH])
            self.dve(lambda e, m=m, pc=pc: e.tensor_copy(out=m[2].ap, in_=pc.ap[:, :NH]), [pc], [m[2]])
            self.dve(lambda e, m=m, ph=ph: e.tensor_scalar(out=m[3].ap, in0=ph.ap[:, :NH], scalar1=-1.0, scalar2=None,
                                                           op0=ALU.mult), [ph], [m[3]])
            self.act(m[4], m[2], AF.Exp, [m[2]])
            self.dve(lambda e, m=m, pl=pl: e.tensor_tensor(out=m[7].ap, in0=pl.ap[:, :NH], in1=m[2].ap, op=ALU.subtract),
                     [pl, m[2]], [m[7]])
            self.act(m[5], m[7], AF.Exp, [m[7]])
            self.dve(lambda e, m=m, pl=pl: e.tensor_copy(out=m[7].ap, in_=pl.ap[:, :NH]), [pl, m[5]], [m[7]])
            self.act(m[6], m[7], AF.Exp, [m[7]])
        if prefix:
            sufs = [Tl(self.af[:, 9, k * NH:(k + 1) * NH], self.afb[9]) for k in range(5)]
            self.dv("memset", [], [sufs[4]], ap=sufs[4].ap, constant=0.0)
            for q in (3, 2, 1, 0):
                mq = self.msm[q]
                self.dv("tensor_tensor", [mq[7], sufs[4]], [sufs[q]], out=sufs[q].ap, in0=mq[7].ap, in1=sufs[4].ap, op=ALU.add)
                self.dv("tensor_tensor", [sufs[4], mq[7]], [sufs[4]], out=sufs[4].ap, in0=sufs[4].ap, in1=mq[7].ap, op=ALU.add)
                self.dv("tensor_tensor", [sufs[q], mq[2]], [sufs[q]], out=sufs[q].ap, in0=sufs[q].ap, in1=mq[2].ap, op=ALU.subtract)
                self.act(mq[5], sufs[q], AF.Exp, [sufs[q]])
            self.act(self.msm[0][6], sufs[4], AF.Exp, [sufs[4]])
        self.ckpt("dt")
        for g in range(NG):
            xTg = [self.pr(11 + i) for i in range(4)]
            BTg, CTg = self.pr(2), self.pr(3)
            for half in range(2):
                sl, wv = self.wload(self.w_in, 0, KC, 4096 + g * 512 + half * 256, 256)
                for mch in range(2):
                    i = half * 2 + mch
                    ps = self.psum()
                    for k in range(KC):
                        self.mm(ps, wv[:, k, mch * 128:(mch + 1) * 128], self.hn[k].ap, k == 0, k == KC - 1,
                                [sl, self.hn[k]])
                    self.conv_chunk(ps, g * 4 + i, xTg[i], xTg[i].ap)
            for (which, dest, cc) in ((0, BTg, 32 + g), (1, CTg, 40 + g)):
                if which == 1 and not need_c:
                    continue
                sl, wv = self.wload(self.w_in, 0, KC, 8192 + which * 1024 + g * 128, 128)
                ps = self.psum()
                for k in range(KC):
                    self.mm(ps, wv[:, k, :], self.hn[k].ap, k == 0, k == KC - 1, [sl, self.hn[k]])
                self.conv_chunk(ps, cc, dest, dest.ap)
            self.ckpt("conv")
            sz = [self.pf(3 + q) for q in range(4)]
            if not prefix:
                zps = [self.psum() for q in range(4)]
                for half in range(2):
                    sl, wv = self.wload(self.w_in, 0, KC, g * 512 + half * 256, 256)
                    for q in range(4):
                        for k in range(KC):
                            self.mm(zps[q], self.hn[k].ap[:, q * 128:(q + 1) * 128], wv[:, k, :], k == 0, k == KC - 1,
                                    [sl, self.hn[k]], out_ap=zps[q].ap[:, half * 256:(half + 1) * 256])
                for q in range(4):
                    self.act(sz[q], zps[q], AF.Silu, [zps[q]])
            yTg = [self.pr(7 + i) for i in range(4)]
            S = self.sstate[g]
            v3 = lambda ap: ap.rearrange("p (h d) -> p h d", d=HP)

            def bcg(t, g=g):
                return t.ap[:, 8 * g:8 * g + 8].unsqueeze(2).broadcast_to([128, 8, HP])

            def front(q, g=g, xTg=xTg, BTg=BTg, CTg=CTg, S=S):
                m = self.msm[q]
                cs = slice(q * 128, (q + 1) * 128)
                ctx = {"q": q, "m": m, "cs": cs}
                psX = self.psum()
                for i in range(4):
                    self.tr(psX, psX.ap[:, i * 128:(i + 1) * 128], xTg[i].ap[:, cs], [xTg[i]])
                Xtok, xdt, xdtw = self.pf(7), self.pr(4), self.pr(5)
                if not prefix:
                    dbc = hv.ap[:, 2 * NH + 8 * g: 2 * NH + 8 * g + 8].unsqueeze(2).broadcast_to([128, 8, HP])
                    self.dv("tensor_tensor", [psX, hv], [Xtok], out=v3(Xtok.ap), in0=v3(psX.ap), in1=dbc, op=ALU.mult)
                self.dv("tensor_tensor", [psX, m[0]], [xdt], out=v3(xdt.ap), in0=v3(psX.ap), in1=bcg(m[0]), op=ALU.mult)
                self.pl("tensor_tensor", [xdt, m[5]], [xdtw], out=v3(xdtw.ap), in0=v3(xdt.ap), in1=bcg(m[5]), op=ALU.mult)
                psB = self.psum()
                self.tr(psB, psB.ap[:, 0:128], BTg.ap[:, cs], [BTg])
                Btok = self.prq(6, 0)
                self.act(Btok, psB, AF.Copy, [psB], in_ap=psB.ap[:, 0:128])
                if not prefix:
                    psC = self.psum()
                    self.mm(psC, BTg.ap[:, cs], CTg.ap[:, cs], True, True, [BTg, CTg], out_ap=psC.ap[:, 0:128])
                    CBT = self.pfq(10, 0)
                    self.dv("tensor_tensor", [psC, self.c_all], [CBT], out=CBT.ap, in0=psC.ap[:, 0:128], in1=self.triu, op=ALU.mult)
                    psY0 = self.psum()
                    self.mm(psY0, CTg.ap[:, cs], S.ap, True, True, [CTg, S])
                    y0 = self.pf(8 + q % 2)
                    self.dv("tensor_tensor", [psY0, m[4]], [y0], out=v3(y0.ap), in0=v3(psY0.ap), in1=bcg(m[4]), op=ALU.mult)
                    self.pl("tensor_tensor", [y0, Xtok], [y0], out=y0.ap, in0=y0.ap, in1=Xtok.ap, op=ALU.add)
                    ctx.update(CBT=CBT, y0=y0, xdt=xdt)
                if prefix:
                    psS = self.ps[6]
                    self.mm(psS, Btok.ap, xdtw.ap, q == 0, q == 3, [Btok, xdtw])
                    if q == 3:
                        m0 = self.msm[0]
                        self.dv("tensor_tensor", [S, m0[6]], [S], out=v3(S.ap), in0=v3(S.ap), in1=bcg(m0[6]), op=ALU.mult)
                        self.dv("tensor_tensor", [S, psS], [S], out=S.ap, in0=S.ap, in1=psS.ap, op=ALU.add)
                else:
                    psS = self.psum()
                    self.mm(psS, Btok.ap, xdtw.ap, True, True, [Btok, xdtw])
                    self.pl("tensor_tensor", [S, m[6]], [S], out=v3(S.ap), in0=v3(S.ap), in1=bcg(m[6]), op=ALU.mult)
                    self.dv("tensor_tensor", [S, psS], [S], out=S.ap, in0=S.ap, in1=psS.ap, op=ALU.add)
                return ctx

            def bc_mm(ctx, g=g):
                q = ctx["q"]
                ahi = self.msr[q][0]
                psEb = [self.psum(), self.psum()]
                ctx["psEb"] = psEb
                for hl in range(8):
                    h = 8 * g + hl
                    pe_, c4 = psEb[hl // 4], (hl % 4) * 128
                    self.mm(pe_, ahi.ap[:, h:h + 1].broadcast_to([128, 128]), self.triu_r.ap, True, True,
                            [ahi, self.triu_r], out_ap=pe_.ap[:, c4:c4 + 128])

            def rest(ctx, g=g):
                q, m, CBT, xdt, psEb = ctx["q"], ctx["m"], ctx["CBT"], ctx["xdt"], ctx["psEb"]
                psYd = self.ps[6 + q % 2]
                ctx["psYd"] = psYd
                for hl in range(8):
                    h = 8 * g + hl
                    pe_, c4 = psEb[hl // 4], (hl % 4) * 128
                    E = self.pfq(10, 1 + hl % 3)
                    self.act(E, pe_, AF.Exp, [pe_, m[3]], bias=m[3].ap[:, h:h + 1], in_ap=pe_.ap[:, c4:c4 + 128])
                    GT = self.prq(6, 1 + hl) if hl < 3 else self.prq(16, (hl - 3) % 4)
                    self.dv("scalar_tensor_tensor", [CBT, E], [GT], out=GT.ap, in0=E.ap, scalar=1.0e30, in1=CBT.ap,
                            op0=ALU.min, op1=ALU.mult)
                    self.mm(psYd, GT.ap, xdt.ap[:, hl * HP:(hl + 1) * HP], True, True, [GT, xdt],
                            out_ap=psYd.ap[:, hl * HP:(hl + 1) * HP])

            def tail_dve(ctx, g=g, sz=sz):
                q, y0, psYd = ctx["q"], ctx["y0"], ctx["psYd"]
                self.dv("tensor_tensor", [y0, psYd], [y0], out=y0.ap, in0=y0.ap, in1=psYd.ap, op=ALU.add)
                szq = sz[q]
                self.dv("tensor_tensor", [y0, szq], [y0], out=y0.ap, in0=y0.ap, in1=szq.ap, op=ALU.mult)
                sm = self.small
                self.dv("memset", [], [sm], ap=sm.ap[:, 0:1], constant=0.0)
                y0a, jka, sma = y0.ap, szq.ap, sm.ap
                self.P.add("act", lambda e, y0a=y0a, jka=jka, sma=sma: e.activation(out=jka, in_=y0a, func=AF.Square,
                                                                                  accum_out=sma[:, 0:1]),
                           reads=[y0, sm], writes=[szq, sm])
                self.dv("tensor_scalar", [sm], [sm], out=sm.ap[:, 1:2], in0=sm.ap[:, 0:1], scalar1=1.0 / 512, scalar2=EPS,
                        op0=ALU.mult, op1=ALU.add)
                self.act(sm, sm, AF.Ln, [sm], out_ap=sm.ap[:, 1:2], in_ap=sm.ap[:, 1:2])
                self.act(sm, sm, AF.Exp, [sm], scale=-0.5, out_ap=sm.ap[:, 2:3], in_ap=sm.ap[:, 1:2])
                yn = self.pr(15)
                ctx["yn"] = yn
                self.act(yn, y0, AF.Copy, [y0, sm], scale=sm.ap[:, 2:3])

            def tail_tr(ctx, g=g, yTg=yTg):
                cs, yn = ctx["cs"], ctx["yn"]
                psT = self.psum()
                for i in range(4):
                    self.tr(psT, psT.ap[:, i * 128:(i + 1) * 128], yn.ap[:, i * 128:(i + 1) * 128], [yn])
                gw = self.gnw
                for i in range(4):
                    self.act(yTg[i], psT, AF.Copy, [psT, gw], scale=gw.ap[:, g * 4 + i: g * 4 + i + 1],
                             out_ap=yTg[i].ap[:, cs], in_ap=psT.ap[:, i * 128:(i + 1) * 128])

            if prefix:
                for q in range(4):
                    front(q)
            else:
                cx = front(0)
                bc_mm(cx)
                rest(cx)
                for q in range(1, 4):
                    nx = front(q)
                    bc_mm(nx)
                    tail_dve(cx)
                    rest(nx)
                    tail_tr(cx)
                    cx = nx
                tail_dve(cx)
                tail_tr(cx)
            if not prefix:
                for half in range(2):
                    sl, wv = self.wload(self.w_out, g * 512, 4, half * 1024, 1024)
                    for cc in range(8):
                        po = self.psum()
                        for k in range(4):
                            self.mm(po, wv[:, k, cc * 128:(cc + 1) * 128], yTg[k].ap, k == 0, k == 3, [sl, yTg[k]])
                        x = self.xres[half * 8 + cc]
                        self.dve(lambda e, x=x, po=po: e.tensor_tensor(out=x.ap, in0=x.ap, in1=po.ap, op=ALU.add), [x, po], [x])


    def head_norm(self, psK, h, colap, dest, rk):
        sqk = self.pr(2 + h % 2)
        self.act(sqk, psK, AF.Square, [psK])
        psR = self.psum()
        self.mm(psR, self.ones_r.ap, sqk.ap, True, True, [self.ones_r, sqk])
        self.dv("tensor_scalar", [psR], [rk], out=rk.ap, in0=psR.ap, scalar1=1.0 / AD, scalar2=EPS, op0=ALU.mult, op1=ALU.add)
        self.act(rk, rk, AF.Ln, [rk])
        self.act(rk, rk, AF.Exp, [rk], scale=-0.5)
        self.dv("scalar_tensor_tensor", [psK, rk], [dest], out=dest.ap, in0=psK.ap, scalar=colap, in1=rk.ap,
                op0=ALU.mult, op1=ALU.mult)

    def kvf_tile(self, j):
        sq = [self.pr(0), self.pr(1)]
        rstd = self.pf(0)
        self.rmsnorm_to_hn(2, sq, rstd)
        kq = self.kqnw
        for hp in range(AH // 2):
            sl, wv = self.wload(self.w_kvf, 0, KC, hp * 256, 256)
            for mch in range(2):
                h = 2 * hp + mch
                psK = self.psum()
                for k in range(KC):
                    self.mm(psK, wv[:, k, mch * 128:(mch + 1) * 128], self.hn[k].ap, k == 0, k == KC - 1, [sl, self.hn[k]])
                kout = self.pf(3 + h % 2)
                self.head_norm(psK, h, kq.ap[:, 0:1], kout, self.pf(1 + h % 2))
                self.dma("sp", self.kt_view(j, h, False), kout.ap, [kout], [self.b_kt[j]], f"kt{h % 2}", accum=True)
        vcnt = 0
        for vb in range(8):
            sl, wv = self.wload(self.w_kvf, 0, KC, D + vb * 256, 256)
            for q in range(4):
                psV = self.psum()
                for k in range(KC):
                    self.mm(psV, self.hn[k].ap[:, q * 128:(q + 1) * 128], wv[:, k, :], k == 0, k == KC - 1,
                            [sl, self.hn[k]], out_ap=psV.ap[:, 0:256])
                p, hf = 5 + (vcnt // 2) % 2, vcnt % 2
                vout = Tl(self.af[:, p, hf * 256:(hf + 1) * 256], [self.afb[p][2 * hf], self.afb[p][2 * hf + 1]])
                self.act(vout, psV, AF.Copy, [psV], in_ap=psV.ap[:, 0:256])
                self.dma("sp", self.v_view(j, q // 2, False)[(q % 2) * 128:(q % 2 + 1) * 128, vb * 256:(vb + 1) * 256], vout.ap,
                         [vout], [self.b_v[j]], f"v{vcnt % 4}", accum=True)
                vcnt += 1
        sl, wv = self.wload(self.w_kvf, 0, KC, 2 * D, AH)
        ks = self.ksm
        ct = self.cumtot
        for q in range(4):
            psF = self.psum()
            for k in range(KC):
                self.mm(psF, self.hn[k].ap[:, q * 128:(q + 1) * 128], wv[:, k, :], k == 0, k == KC - 1,
                        [sl, self.hn[k]], out_ap=psF.ap[:, 0:AH])
            self.dv("tensor_tensor", [psF, self.bfv], [ks[0]], out=ks[0].ap, in0=psF.ap[:, 0:AH], in1=self.bfv.ap, op=ALU.add)
            self.softplus_small(ks[0], ks[1], ks[2], ks[3], AH, neg=True)
            lhi, llo = self.msr[q]
            lha, lla = lhi.ap[:, 0:AH], llo.ap[:, 0:AH]
            self.dv("tensor_copy", [ks[3]], [lhi], out=lha, in_=ks[3].ap)
            self.dv("tensor_tensor", [ks[3], lhi], [llo], out=lla, in0=ks[3].ap, in1=lha.bitcast(F32), op=ALU.subtract)
            psC, psT = self.psum(), self.psum()
            tr_, onr = self.triu_r, self.ones_r
            self.mm(psC, tr_.ap, lha, True, False, [tr_, lhi], out_ap=psC.ap[:, 0:AH])
            self.mm(psC, tr_.ap, lla, False, True, [tr_, llo], out_ap=psC.ap[:, 0:AH])
            self.mm(psT, onr.ap, lha, True, False, [onr, lhi], out_ap=psT.ap[:, 0:AH])
            self.mm(psT, onr.ap, lla, False, True, [onr, llo], out_ap=psT.ap[:, 0:AH])
            self.dv("tensor_tensor", [psC, ct], [ks[4]], out=ks[4].ap, in0=psC.ap[:, 0:AH], in1=ct.ap, op=ALU.add)
            self.dv("tensor_tensor", [psT, ct], [ct], out=ct.ap, in0=ct.ap, in1=psT.ap[:, 0:AH], op=ALU.add)
            self.dma("sp", self.cum_view(j, False)[q * 128:(q + 1) * 128, :], ks[4].ap, [ks[4]], [self.b_cum[j]],
                     "cu", accum=True)
            chi = self.prq(4, q)
            self.dv("tensor_copy", [ks[4]], [chi], out=chi.ap[:, 0:AH], in_=ks[4].ap)
            psX = self.psum()
            self.tr(psX, psX.ap[0:AH, 0:128], chi.ap[:, 0:AH], [chi])
            cts = self.cumTsb
            self.act(cts, psX, AF.Copy, [psX], out_ap=cts.ap[:, q * 128:(q + 1) * 128], in_ap=psX.ap[0:AH, 0:128])
        self.dma("sp", self.cumTd[:, j * T:(j + 1) * T], self.cumTsb.ap, [self.cumTsb], [self.b_cumT[j]], "ct")
        groups = [[2 * i, 2 * i + 1] for i in range(self.cfg.get("ncores", 8) // 2)]
        pairs = [(self.packK[j][0], self.gathK[j][0], [self.b_kt[j]]), (self.packK[j][1], self.gathK[j][1], [self.b_kt[j]]),
                 (self.packV[j][0], self.gathV[j][0], [self.b_v[j]]), (self.packV[j][1], self.gathV[j][1], [self.b_v[j]]),
                 (self.packC[j], self.gathC[j], [self.b_cum[j]])]
        for ci, (pk, gt, dep) in enumerate(pairs):
            pk2 = pk.rearrange("(r c) -> r c", r=128)
            gt2 = gt.rearrange("(r c) -> r c", r=256)
            self.P.add("pool", lambda e, pk2=pk2, gt2=gt2: e.collective_compute(
                "AllGather", ALU.bypass, replica_groups=groups, ins=[pk2], outs=[gt2]),
                reads=dep, writes=[self.b_g[j]], stream=f"cc{ci}", inc=1, accum=True)

    def make_nb(self):
        nt = self.NT
        nb = self.pf(10)
        self.nb = nb
        self.nbv = nb.ap[:, 0:4 * nt * AH].rearrange("p (k h) -> p k h", h=AH)
        self.nbpv = nb.ap[:, 256:256 + 4 * nt * AH].rearrange("p (k h) -> p k h", h=AH)
        for j in range(nt):
            self.dma("sp", self.nbv[:, 4 * j:4 * j + 4, :], self.cum_view(j, False).rearrange("(k p) h -> p k h", p=128),
                     [self.b_cum[j]], [nb], "nb", accum=True)
            self.dma("sp", self.nbpv[:, 4 * j:4 * j + 4, :], self.cum_view(j, True).rearrange("(k p) h -> p k h", p=128),
                     [self.b_g[j]], [nb], "nb", accum=True)
        tb, fl = self.totb, self.flags
        last = self.gathC[nt - 1][(T - 1) * AH: T * AH]
        self.dma("sp", tb.ap, last.partition_broadcast(128), [self.b_g[nt - 1]], [tb], "tb")
        self.dv("tensor_scalar", [tb, fl], [tb], out=tb.ap, in0=tb.ap, scalar1=fl.ap[:, 1:2], scalar2=None, op0=ALU.add)
        self.dv("tensor_scalar", [nb], [nb], out=nb.ap, in0=nb.ap, scalar1=-1.0, scalar2=None, op0=ALU.mult)
        self.dv("tensor_tensor", [nb, tb], [nb], out=self.nbpv, in0=self.nbpv,
                in1=tb.ap.unsqueeze(1).broadcast_to([128, 4 * nt, AH]), op=ALU.add)

    def attn_tile(self, j):
        sq = [self.pr(0), self.pr(1)]
        rstd = self.pf(0)
        self.rmsnorm_to_hn(3, sq, rstd)
        qc = self.qcol
        for hp in range(AH // 2):
            sl, wv = self.wload(self.w_q, 0, KC, hp * 256, 256)
            for mch in range(2):
                h = 2 * hp + mch
                psQ = self.psum()
                for k in range(KC):
                    self.mm(psQ, wv[:, k, mch * 128:(mch + 1) * 128], self.hn[k].ap, k == 0, k == KC - 1, [sl, self.hn[k]])
                qout = self.pf(3 + h % 2)
                self.head_norm(psQ, h, qc.ap[:, 0:1], qout, self.pf(1 + h % 2))
                self.dma("sp", self.qTd[h], qout.ap, [qout], [self.b_q[h]], f"q{h % 2}")
        onr, idr, nmr = self.ones_r, self.ident_r, self.negmask_r
        psO, psL = self.ps[6], self.ps[7]
        npv = 4 * self.NT
        nkb = npv + 4 * (j + 1)
        scnt = 0
        ecnt = 0
        for h in range(AH):
            qT = self.pr(2 + h % 2)
            self.dma("pool", qT.ap, self.qTd[h], [self.b_q[h]], [qT], f"lq{h % 2}")
            crow = self.pr(4 + h % 2)
            self.dma("pool", crow.ap, self.cumTd[h, j * T:(j + 1) * T].partition_broadcast(128), [self.b_cumT[j]], [crow],
                     f"lc{h % 2}")
            self.dv("tensor_scalar", [crow], [crow], out=crow.ap, in0=crow.ap, scalar1=1.0 / 128, scalar2=None, op0=ALU.mult)
            items = []
            for sst in range(self.NT + j + 1):
                prev = sst < self.NT
                st = sst if prev else sst - self.NT
                for i in range(4):
                    items.append((sst, prev, st, i))
            stage_tiles = {}

            def emit_s(it):
                nonlocal scnt, ecnt
                sst, prev, st, i = it
                if sst not in stage_tiles:
                    Kst = self.pr(6 + scnt % 4)
                    Vst = self.pr(10 + scnt % 4)
                    sdep = [self.b_g[st]] if prev else [self.b_kt[st], self.b_v[st]]
                    self.dma("pool", Kst.ap, self.kt_view(st, h, prev), sdep, [Kst], f"lk{scnt % 4}")
                    vv = Vst.ap.rearrange("p (i d) -> p i d", d=AD)
                    for vh in range(2):
                        self.dma("pool", vv[:, 2 * vh:2 * vh + 2, :],
                                 self.v_view(st, vh, prev)[:, h * AD:(h + 1) * AD].rearrange("(i p) d -> p i d", p=128),
                                 sdep, [Vst], f"lv{scnt % 4}", accum=(vh == 1))
                    scnt += 1
                    stage_tiles[sst] = (Kst, Vst, vv)
                Kst, Vst, vv = stage_tiles[sst]
                diag = (not prev) and (st == j)
                nbsel = self.nbpv if prev else self.nbv
                c0 = i * 128 if diag else 0
                psS = self.psum()
                self.mm(psS, onr.ap, crow.ap[:, c0:], True, False, [onr, crow], out_ap=psS.ap[:, c0:])
                self.mm(psS, Kst.ap[:, i * 128:(i + 1) * 128], qT.ap[:, c0:], False, not diag, [Kst, qT], out_ap=psS.ap[:, c0:])
                if diag:
                    self.mm(psS, idr.ap, nmr.ap, False, True, [idr, nmr], out_ap=psS.ap[:, c0:c0 + 128])
                ET = self.pr(ecnt % 2)
                ecnt += 1
                self.act(ET, psS, AF.Exp, [psS, self.nb], bias=nbsel[:, 4 * st + i, h:h + 1], out_ap=ET.ap[:, c0:],
                         in_ap=psS.ap[:, c0:])
                return (ET, c0, Vst, vv, i, 4 * sst + i)

            esum = self.pf(6)

            def emit_pv(p):
                ET, c0, Vst, vv, i, kb = p
                self.mm(psO, vv[:, i, :], ET.ap[:, c0:], kb == 0, kb == nkb - 1, [Vst, ET], out_ap=psO.ap[:, c0:])
                if kb == 0:
                    self.dv("tensor_copy", [ET], [esum], out=esum.ap, in_=ET.ap.bitcast(F32))
                else:
                    self.dv("tensor_tensor", [ET, esum], [esum], out=esum.ap[:, c0:], in0=esum.ap[:, c0:],
                            in1=ET.ap[:, c0:].bitcast(F32), op=ALU.add)

            pend = None
            for it in items:
                cur = emit_s(it)
                if pend is not None:
                    emit_pv(pend)
                pend = cur
            emit_pv(pend)
            esr = self.pr(14)
            self.dv("tensor_copy", [esum], [esr], out=esr.ap, in_=esum.ap)
            self.mm(psL, onr.ap, esr.ap, True, True, [onr, esr])
            rl = self.pf(5)
            self.dv("reciprocal", [psL], [rl], out=rl.ap, in_=psL.ap)
            oT = self.hn[h]
            self.dv("tensor_tensor", [psO, rl], [oT], out=oT.ap, in0=psO.ap, in1=rl.ap, op=ALU.mult)
        for cb in range(8):
            sl, wv = self.wload(self.w_o, 0, KC, cb * 256, 256)
            for mch in range(2):
                po = self.psum()
                for k in range(KC):
                    self.mm(po, wv[:, k, mch * 128:(mch + 1) * 128], self.hn[k].ap, k == 0, k == KC - 1, [sl, self.hn[k]])
                x = self.xres[2 * cb + mch]
                self.dv("tensor_tensor", [x, po], [x], out=x.ap, in0=x.ap, in1=po.ap, op=ALU.add)

    def load_tile(self, src, j, dep=None):
        for c in range(KC):
            x = self.xres[c]
            self.dma("sp", x.ap, src[c, :, j * T:(j + 1) * T], [dep] if dep else [], [x], f"x{c}")

    def store_tile(self, dst, j, tag=None):
        for c in range(KC):
            x = self.xres[c]
            self.dma("sp", dst[c, :, j * T:(j + 1) * T], x.ap, [x], [tag] if tag else [], f"o{c}", accum=True)


def _build(S, cfg):
    B = Builder(S, cfg)
    B.load_consts()
    mode = cfg.get("mode", "full")
    if mode == "full":
        nt = cfg.get("ntiles", B.NT)
        for j in range(nt if cfg.get("prefix_pass", True) else 0):
            B.load_tile(B.xpT, j)
            B.mamba_tile(prefix=True, need_c=(j == nt - 1))
        fl = B.flags
        for g in range(NG):
            st_ = B.sstate[g]
            B.dv("tensor_scalar", [st_, fl], [st_], out=st_.ap, in0=st_.ap, scalar1=fl.ap[:, 0:1], scalar2=None, op0=ALU.mult)
        for c in range(48):
            cr = B.carry[c]
            B.dv("tensor_scalar", [cr, fl], [cr], out=cr.ap, in0=cr.ap, scalar1=fl.ap[:, 0:1], scalar2=None, op0=ALU.mult)
        for j in range(nt):
            B.load_tile(B.xT, j)
            B.mamba_tile()
            B.ffn(0)
            B.kvf_tile(j)
            B.store_tile(B.h1T, j, tag=B.b_h1[j])
        B.make_nb()
        for j in range(nt if cfg.get("sweep2", True) else 0):
            B.load_tile(B.h1T, j, dep=B.b_h1[j])
            if cfg.get("attn", True):
                B.attn_tile(j)
            if cfg.get("ffn1", True):
                B.ffn(1)
            B.store_tile(B.outT, j)
    if mode == "mamba_only":
        for j in range(cfg.get("ntiles", 1)):
            B.load_tile(B.xT, j)
            try:
                B.mamba_tile(prefix=cfg.get("prefix", False))
            except StopBuild:
                pass
            B.store_tile(B.outT, j)
    if mode == "ffn_only":
        for j in range(cfg.get("ntiles", 1)):
            B.load_tile(B.xT, j)
            B.ffn(0)
            B.store_tile(B.outT, j)
    B.P.emit()
    return B


def _layout_inputs(inp):
    f = np.float32
    col = lambda v: np.ascontiguousarray(np.asarray(v, f).reshape(-1, 128).T)
    rep = lambda v: np.ascontiguousarray(np.broadcast_to(np.asarray(v, f).reshape(1, -1), (128, np.asarray(v).size)))
    normw = np.concatenate([col(inp["a_norm_w"][0]), col(inp["ffn_norm_w"][0]), col(inp["kv_norm_w"]),
                            col(inp["b_norm_w"][0]), col(inp["ffn_norm_w"][1])], axis=1)
    cw = np.asarray(inp["a_conv_w"][0], f)
    convw = np.ascontiguousarray(cw.reshape(4, 48, 128).transpose(2, 1, 0).reshape(128, 48 * 4))
    ident = np.eye(128, dtype=f)
    triu = np.triu(np.ones((128, 128), f))
    negmask = (np.tril(np.ones((128, 128), f), -1) * NEG).astype(f)
    shared = {
        "w_in": np.ascontiguousarray(inp["a_in_proj"][0], f), "w_out": np.ascontiguousarray(inp["a_out_proj"][0], f),
        "w_gu0": np.ascontiguousarray(inp["w_gate_up"][0], f), "w_gu1": np.ascontiguousarray(inp["w_gate_up"][1], f),
        "w_dn0": np.ascontiguousarray(inp["w_down"][0], f), "w_dn1": np.ascontiguousarray(inp["w_down"][1], f),
        "w_kvf": np.ascontiguousarray(inp["w_kvf"], f), "w_q": np.ascontiguousarray(inp["w_q"][0], f),
        "w_o": np.ascontiguousarray(inp["w_o"][0], f),
        "normw": np.ascontiguousarray(normw), "gnw": col(inp["a_gnorm_w"][0]),
        "kqnw": np.ascontiguousarray(np.stack([np.asarray(inp["k_norm_w"], f), np.asarray(inp["q_norm_w"][0], f)], axis=1)),
        "convw": convw, "convb": col(inp["a_conv_b"][0]),
        "hvec": np.concatenate([rep(inp["a_dt_bias"][0]), rep(inp["a_A_log"][0]), rep(inp["a_D"][0])], axis=1),
        "bfv": rep(inp["b_f"]),
        "consts": np.concatenate([ident, triu, negmask], axis=1),
    }
    return shared


def _xT(xb):
    S = xb.shape[0]
    return np.ascontiguousarray(np.asarray(xb, np.float32).T.reshape(KC, 128, S))


def kernel(**inputs):
    x = np.asarray(inputs["x"], np.float32)
    nb, S, _ = x.shape
    H = S // 2
    shared = _layout_inputs(inputs)
    B = _build(H, {"mode": "full"})
    in_maps = []
    for b in range(nb):
        first = _xT(x[b, :H])
        for half in range(2):
            m = dict(shared)
            m["xT"] = first if half == 0 else _xT(x[b, H:])
            m["xpT"] = first
            fl = np.zeros((128, 2), np.float32)
            fl[:, 0] = float(half)
            fl[:, 1] = 0.0 if half == 1 else NEG
            m["flags"] = fl
            in_maps.append(m)
    res = run_bass_kernel_spmd(B.nc, in_maps, core_ids=list(range(2 * nb)))
    out = np.empty((nb, S, D), np.float32)
    for b in range(nb):
        for half in range(2):
            o = res.results[2 * b + half]["outT"]
            out[b, half * H:(half + 1) * H] = o.reshape(D, H).T
    return out
```

```python
import contextlib
import numpy as np
import concourse.bass as bass
import concourse.mybir as mybir
from concourse.bass_utils import run_bass_kernel_spmd

F32 = mybir.dt.float32
F32R = mybir.dt.float32r
AF = mybir.ActivationFunctionType
ALU = mybir.AluOpType
AX = mybir.AxisListType

D = 2048
DI = 4096
NH = 64
HP = 64
NG = 8
NS = 128
DXBC = 6144
DINP = 10304
DFF = 5632
AH = 16
AD = 128
T = 512
KC = D // 128
EPS = 1e-6
NEG = -30000.0


class Buf:
    __slots__ = ("name", "w", "r")

    def __init__(self, name=""):
        self.name = name
        self.w = []
        self.r = {}


class Tl:
    __slots__ = ("ap", "bufs")

    def __init__(self, ap, bufs=None, name=""):
        self.ap = ap
        self.bufs = bufs if bufs is not None else [Buf(name)]


class Op:
    __slots__ = ("eng", "fn", "deps", "idx", "sig", "sem", "semval", "dma", "stream", "inc")


class Prog:
    ENGS = ("pe", "act", "dve", "pool", "sp")

    def __init__(self, nc):
        self.nc = nc
        self.ops = []
        self.streams = {}

    def add(self, eng, fn, reads=(), writes=(), stream=None, accum=False, inc=16):
        op = Op()
        op.inc = inc
        op.eng, op.fn, op.idx = eng, fn, len(self.ops)
        op.dma = stream is not None
        op.stream = stream
        op.sig = op.dma
        deps = set()
        rb = [b for t in reads for b in t.bufs]
        wb = [b for t in writes for b in t.bufs]
        for b in rb:
            deps.update(b.w)
        for b in wb:
            deps.update(b.w)
            deps.update(b.r.values())
        op.deps = deps
        rkey = (eng, stream) if stream is not None else eng
        for b in rb:
            b.r[rkey] = op.idx
        for b in wb:
            if accum:
                b.w.append(op.idx)
            else:
                b.w = [op.idx]
            b.r = {}
        if op.dma:
            st = self.streams.setdefault(stream, {"count": 0})
            st["count"] += inc
            op.semval = st["count"]
        self.ops.append(op)
        return op

    def emit(self):
        nc = self.nc
        ops = self.ops
        for op in ops:
            for d in op.deps:
                dop = ops[d]
                if dop.dma:
                    continue
                if dop.eng == op.eng and op.eng == "pe" and not op.dma:
                    continue
                dop.sig = True
        with contextlib.ExitStack() as es:
            esem = {e: es.enter_context(nc.semaphore("s_" + e)) for e in self.ENGS}
            ssem = {s: es.enter_context(nc.semaphore("d_" + s)) for s in self.streams}
            cnt = {e: 0 for e in self.ENGS}
            for op in ops:
                if op.dma:
                    op.sem = ssem[op.stream]
                    if op.stream.startswith("const"):
                        op.semval = self.streams[op.stream]["count"]
                else:
                    op.sem = esem[op.eng]
                    if op.sig:
                        cnt[op.eng] += 1
                        op.semval = cnt[op.eng]
            block = es.enter_context(nc.Block())
            per = {e: [o for o in ops if o.eng == e] for e in self.ENGS}
            final_waits = [(ssem[s], st["count"]) for s, st in self.streams.items()]

            def run(e, ename):
                seen = {}
                for op in per[ename]:
                    for d in sorted(op.deps):
                        dop = ops[d]
                        if (not dop.dma) and dop.eng == ename and ename == "pe" and not op.dma:
                            continue
                        key = id(dop.sem)
                        if seen.get(key, 0) >= dop.semval:
                            continue
                        e.wait_ge(dop.sem, dop.semval)
                        seen[key] = dop.semval
                    ins = op.fn(e)
                    if op.sig:
                        ins.then_inc(op.sem, op.inc if op.dma else 1)
                if ename == "sp":
                    for s, v in final_waits:
                        e.wait_ge(s, v)

            @block.tensor
            def _(e):
                run(e, "pe")

            @block.scalar
            def _(e):
                run(e, "act")

            @block.vector
            def _(e):
                run(e, "dve")

            @block.gpsimd
            def _(e):
                run(e, "pool")

            @block.sync
            def _(e):
                run(e, "sp")


class StopBuild(Exception):
    pass


class Builder:
    def ckpt(self, name):
        if self.cfg.get("stop") == name:
            raise StopBuild(name)

    def __init__(self, S, cfg):
        self.S = S
        self.cfg = cfg
        self.NT = S // T
        nc = bass.Bass("TRN2", target_bir_lowering=False)
        self.nc = nc
        self.P = Prog(nc)
        self.dram_in = {}
        self._mk_dram()
        self._mk_sbuf()

    def din(self, name, shape):
        t = self.nc.dram_tensor(name, list(shape), F32, kind="ExternalInput").ap()
        self.dram_in[name] = t
        return t

    def _mk_dram(self):
        S = self.S
        nc = self.nc
        self.xT = self.din("xT", [KC, 128, S])
        self.w_in = self.din("w_in", [D, DINP])
        self.w_out = self.din("w_out", [DI, D])
        self.w_gu = [self.din("w_gu0", [D, 2 * DFF]), self.din("w_gu1", [D, 2 * DFF])]
        self.w_dn = [self.din("w_dn0", [DFF, D]), self.din("w_dn1", [DFF, D])]
        self.w_kvf = self.din("w_kvf", [D, 2 * D + AH])
        self.w_q = self.din("w_q", [D, D])
        self.w_o = self.din("w_o", [D, D])
        self.d_normw = self.din("normw", [128, 5 * KC])
        self.d_gnw = self.din("gnw", [128, DI // 128])
        self.d_kqnw = self.din("kqnw", [128, 2])
        self.d_convw = self.din("convw", [128, 48 * 4])
        self.d_convb = self.din("convb", [128, 48])
        self.d_hv = self.din("hvec", [128, 3 * NH])
        self.d_bf = self.din("bfv", [128, AH])
        self.d_consts = self.din("consts", [128, 3 * 128])
        self.outT = nc.dram_tensor("outT", [KC, 128, S], F32, kind="ExternalOutput").ap()
        self.xpT = self.din("xpT", [KC, 128, S])
        self.d_flags = self.din("flags", [128, 2])
        self.h1T = nc.dram_tensor("h1T", [KC, 128, S], F32).ap()
        self.cumTd = nc.dram_tensor("cumTd", [AH, S], F32).ap()
        self.PKh = (AH // 2) * 128 * T
        self.PVh = (T // 2) * D
        self.PC = T * AH
        mk = lambda n, sz: nc.dram_tensor(n, [sz], F32).ap()
        self.packK = [[mk(f"pK{j}_{i}", self.PKh) for i in range(2)] for j in range(self.NT)]
        self.gathK = [[mk(f"gK{j}_{i}", 2 * self.PKh) for i in range(2)] for j in range(self.NT)]
        self.packV = [[mk(f"pV{j}_{i}", self.PVh) for i in range(2)] for j in range(self.NT)]
        self.gathV = [[mk(f"gV{j}_{i}", 2 * self.PVh) for i in range(2)] for j in range(self.NT)]
        self.packC = [mk(f"pC{j}", self.PC) for j in range(self.NT)]
        self.gathC = [mk(f"gC{j}", 2 * self.PC) for j in range(self.NT)]
        self.b_g = [Tl(None, name=f"g_{j}") for j in range(self.NT)]
        self.qTd = nc.dram_tensor("qTd", [AH, 128, T], F32).ap()
        self.b_q = [Tl(None, name=f"q_{h}") for h in range(AH)]
        nt = self.NT
        self.b_h1 = [Tl(None, name=f"h1_{j}") for j in range(nt)]
        self.b_kt = [Tl(None, name=f"kt_{j}") for j in range(nt)]
        self.b_v = [Tl(None, name=f"v_{j}") for j in range(nt)]
        self.b_cum = [Tl(None, name=f"cum_{j}") for j in range(nt)]
        self.b_cumT = [Tl(None, name=f"cumT_{j}") for j in range(nt)]

    def kt_view(self, j, h, prev):
        flat = (self.gathK if prev else self.packK)[j][h // 8]
        return flat[0:self.PKh].rearrange("(h d t) -> h d t", h=AH // 2, d=128)[h % 8]

    def v_view(self, j, vh, prev):
        flat = (self.gathV if prev else self.packV)[j][vh]
        return flat[0:self.PVh].rearrange("(t c) -> t c", c=D)

    def cum_view(self, j, prev):
        flat = (self.gathC if prev else self.packC)[j]
        return flat[0:self.PC].rearrange("(t h) -> t h", h=AH)

    def sb(self, name, shape, dt=F32):
        return self.nc.alloc_sbuf_tensor("sb_" + name, list(shape), dt).ap()

    def _mk_sbuf(self):
        nc = self.nc
        xr = self.sb("xres", [128, KC, T])
        self.xres = [Tl(xr[:, c, :], name=f"xres{c}") for c in range(KC)]
        hn = self.sb("hn", [128, KC, T], F32R)
        self.hn = [Tl(hn[:, c, :], name=f"hn{c}") for c in range(KC)]
        self.NW = 3
        self.wsl = [Tl(self.sb(f"wsl{i}", [128, 4096], F32R), name=f"wsl{i}") for i in range(self.NW)]
        self.wrr = 0
        cst = self.sb("consts", [128, 3 * 128])
        self.c_all = Tl(cst, name="consts")
        self.ident = cst[:, 0:128]
        self.triu = cst[:, 128:256]
        self.negmask = cst[:, 256:384]
        self.ones_r = Tl(self.sb("ones_r", [128, 128], F32R), name="ones_r")
        self.normw = Tl(self.sb("normw", [128, 5 * KC]), name="normw")
        self.gnw = Tl(self.sb("gnw", [128, 32]), name="gnw")
        self.kqnw = Tl(self.sb("kqnw", [128, 2]), name="kqnw")
        self.convw = Tl(self.sb("convw", [128, 48 * 4]), name="convw")
        self.convb = Tl(self.sb("convb", [128, 48]), name="convb")
        self.hv = Tl(self.sb("hvec", [128, 3 * NH]), name="hvec")
        self.bfv = Tl(self.sb("bfv", [128, AH]), name="bfv")
        self.flags = Tl(self.sb("flags", [128, 2]), name="flags")
        self.totb = Tl(self.sb("totb", [128, AH]), name="totb")
        self.Aneg = Tl(self.sb("Aneg", [128, NH]), name="Aneg")
        st = self.sb("sstate", [128, NG, 512], F32R)
        self.sstate = [Tl(st[:, g, :], name=f"S{g}") for g in range(NG)]
        cr = self.sb("carry", [128, 48, 4])
        self.carry = [Tl(cr[:, c, 0:3], name=f"carry{c}") for c in range(48)]
        self.cumtot = Tl(self.sb("cumtot", [128, AH]), name="cumtot")
        msm = self.sb("msm", [128, 4, 8, NH])
        self.msm = [[Tl(msm[:, q, k, :], name=f"msm{q}_{k}") for k in range(8)] for q in range(4)]
        ub = self.sb("ubuf", [128, 2, 516])
        self.ubuf = [Tl(ub[:, i, :], name=f"ubuf{i}") for i in range(2)]
        self.urr = 0
        msr = self.sb("msr", [128, 4, 2, NH], F32R)
        self.msr = [[Tl(msr[:, q, k, :], name=f"msr{q}_{k}") for k in range(2)] for q in range(4)]
        self.triu_r = Tl(self.sb("triu_r", [128, 128], F32R), name="triu_r")
        self.ident_r = Tl(self.sb("ident_r", [128, 128], F32R), name="ident_r")
        self.negmask_r = Tl(self.sb("negmask_r", [128, 128], F32R), name="negmask_r")
        self.qcol = Tl(self.sb("qcol", [128, 1]), name="qcol")
        self.small = Tl(self.sb("small", [128, 8]), name="small")
        ksm = self.sb("ksm", [128, 8, AH])
        self.ksm = [Tl(ksm[:, k, :], name=f"ksm{k}") for k in range(8)]
        self.cumTsb = Tl(self.sb("cumTsb", [AH, T]), name="cumTsb")

        self.NPF = 11
        self.NPR = 17
        af = self.sb("arenaF", [128, self.NPF, 512])
        ar = self.sb("arenaR", [128, self.NPR, 512], F32R)
        self.af, self.ar = af, ar
        self.afb = [[Buf(f"af{p}_{q}") for q in range(4)] for p in range(self.NPF)]
        self.arb = [[Buf(f"ar{p}_{q}") for q in range(4)] for p in range(self.NPR)]
        ps = [nc.alloc_psum_tensor(f"ps{i}", [128, 512], F32).ap() for i in range(8)]
        self.ps = [Tl(ps[i], name=f"ps{i}") for i in range(8)]
        self.psrr = 0

    def pf(self, p, n=1):
        ap = self.af[:, p:p + n, :].rearrange("p a b -> p (a b)")
        return Tl(ap, [b for i in range(p, p + n) for b in self.afb[i]])

    def pfq(self, p, q, cols=128):
        return Tl(self.af[:, p, q * 128:q * 128 + cols], [self.afb[p][q]])

    def pr(self, p, n=1):
        ap = self.ar[:, p:p + n, :].rearrange("p a b -> p (a b)")
        return Tl(ap, [b for i in range(p, p + n) for b in self.arb[i]])

    def prq(self, p, q, cols=128):
        return Tl(self.ar[:, p, q * 128:q * 128 + cols], [self.arb[p][q]])

    def psum(self):
        t = self.ps[self.psrr % 6]
        self.psrr += 1
        return t

    def mm(self, out, lhsT, rhs, start, stop, reads, out_ap=None):
        oap = out.ap if out_ap is None else out_ap
        self.P.add("pe", lambda e: e.matmul(oap, lhsT=lhsT, rhs=rhs, start=start, stop=stop),
                   reads=reads, writes=[out])

    def tr(self, out, out_ap, in_ap, reads):
        kk = in_ap.shape[0]
        ir = self.ident_r
        self.P.add("pe", lambda e: e.matmul(out_ap, lhsT=in_ap, rhs=ir.ap[:kk, :kk], start=True, stop=True),
                   reads=reads + [ir], writes=[out])

    def act(self, out, in_, func, reads, bias=None, scale=None, out_ap=None, in_ap=None, extra_w=()):
        oap = out.ap if out_ap is None else out_ap
        iap = in_.ap if in_ap is None else in_ap
        kw = {}
        if bias is not None:
            kw["bias"] = bias
        if scale is not None:
            kw["scale"] = scale
        self.P.add("act", lambda e: e.activation(out=oap, in_=iap, func=func, **kw),
                   reads=reads, writes=[out] + list(extra_w))

    def dve(self, fn, reads, writes):
        self.P.add("dve", fn, reads=reads, writes=writes)

    def pl(self, name, reads, writes, **kw):
        eng = "pool" if self.cfg.get("pool_ops", False) else "dve"
        self.P.add(eng, lambda e: getattr(e, name)(**kw), reads=reads, writes=writes)

    def dv(self, name, reads, writes, **kw):
        self.P.add("dve", lambda e: getattr(e, name)(**kw), reads=reads, writes=writes)

    def dma(self, eng, out_ap, in_ap, reads, writes, stream, nonc=False, accum=False):
        if nonc:
            self.P.add(eng, lambda e: e.dma_start(out=out_ap, in_=in_ap, allow_slow_non_contiguous=True),
                       reads=reads, writes=writes, stream=stream, accum=accum)
        else:
            self.P.add(eng, lambda e: e.dma_start(out=out_ap, in_=in_ap), reads=reads, writes=writes, stream=stream,
                       accum=accum)

    def wload(self, W, r0, nk, c0, ncols):
        i = self.wrr % self.NW
        self.wrr += 1
        slot = self.wsl[i]
        src = W[r0:r0 + nk * 128, c0:c0 + ncols].rearrange("(k p) n -> p k n", p=128)
        dst = slot.ap[:, 0:nk * ncols].rearrange("p (k n) -> p k n", k=nk)
        self.dma("pool", dst, src, [], [slot], f"w{i}")
        return slot, dst

    def load_consts(self):
        P = self.P
        pairs = [(self.c_all, self.d_consts), (self.normw, self.d_normw), (self.gnw, self.d_gnw),
                 (self.kqnw, self.d_kqnw), (self.convw, self.d_convw), (self.convb, self.d_convb),
                 (self.hv, self.d_hv), (self.bfv, self.d_bf), (self.flags, self.d_flags)]
        for t, d in pairs:
            self.dma("sp", t.ap, d, [], [t], "const")
        o_r = self.ones_r
        zf = self.pf(0)
        self.dve(lambda e: e.memset(zf.ap, 1.0), [], [zf])
        self.dve(lambda e: e.tensor_copy(out=o_r.ap, in_=zf.ap[:, 0:128]), [zf], [o_r])
        self.dve(lambda e: e.memset(zf.ap, 0.0), [zf], [zf])
        hv, an = self.hv, self.Aneg
        self.act(an, hv, AF.Exp, [hv], in_ap=hv.ap[:, NH:2 * NH])
        self.dve(lambda e: e.tensor_scalar(out=an.ap, in0=an.ap, scalar1=-1.0, scalar2=None, op0=ALU.mult), [an], [an])
        for g in range(NG):
            s = self.sstate[g]
            self.dve(lambda e, s=s: e.tensor_copy(out=s.ap, in_=zf.ap), [zf], [s])
        for c in range(48):
            cr = self.carry[c]
            self.dve(lambda e, cr=cr: e.memset(cr.ap, 0.0), [], [cr])
        ct = self.cumtot
        self.dve(lambda e: e.memset(ct.ap, 0.0), [], [ct])
        ir, nr, ca = self.ident_r, self.negmask_r, self.c_all
        self.dve(lambda e: e.tensor_copy(out=ir.ap, in_=self.ident), [ca], [ir])
        self.dve(lambda e: e.tensor_copy(out=nr.ap, in_=self.negmask), [ca], [nr])
        tr_ = self.triu_r
        self.dve(lambda e: e.tensor_copy(out=tr_.ap, in_=self.triu), [ca], [tr_])
        qc, kq = self.qcol, self.kqnw
        self.dve(lambda e: e.tensor_scalar(out=qc.ap, in0=kq.ap[:, 1:2], scalar1=float(AD) ** -0.5, scalar2=None,
                                           op0=ALU.mult), [kq], [qc])

    def rstd_from_chunks(self, srcs, src_aps, nfeat, sq_tiles, rstd):
        n = len(srcs)
        ps = self.psum()
        ncol = src_aps[0].shape[-1]
        for c in range(n):
            sq = sq_tiles[c % 2]
            self.act(sq, srcs[c], AF.Square, [srcs[c]], out_ap=sq.ap[:, :ncol], in_ap=src_aps[c])
            self.mm(ps, self.ones_r.ap, sq.ap[:, :ncol], c == 0, c == n - 1, [sq, self.ones_r],
                    out_ap=ps.ap[:, :ncol])
        ra = rstd.ap[:, :ncol]
        pa = ps.ap[:, :ncol]
        self.dve(lambda e: e.tensor_scalar(out=ra, in0=pa, scalar1=1.0 / nfeat, scalar2=EPS,
                                           op0=ALU.mult, op1=ALU.add), [ps], [rstd])
        self.act(rstd, rstd, AF.Ln, [rstd], out_ap=ra, in_ap=ra)
        self.act(rstd, rstd, AF.Exp, [rstd], scale=-0.5, out_ap=ra, in_ap=ra)

    def rmsnorm_to_hn(self, widx, sq_tiles, rstd):
        self.rstd_from_chunks(self.xres, [x.ap for x in self.xres], D, sq_tiles, rstd)
        nw = self.normw
        for c in range(KC):
            x, h = self.xres[c], self.hn[c]
            col = nw.ap[:, widx * KC + c: widx * KC + c + 1]
            self.dve(lambda e, x=x, h=h, col=col: e.scalar_tensor_tensor(
                out=h.ap, in0=x.ap, scalar=col, in1=rstd.ap, op0=ALU.mult, op1=ALU.mult),
                [x, rstd, nw], [h])

    def ffn(self, layer):
        sq = [self.pr(0), self.pr(1)]
        rstd = self.pf(0)
        self.rmsnorm_to_hn(1 if layer == 0 else 4, sq, rstd)
        wgu, wdn = self.w_gu[layer], self.w_dn[layer]
        nblk = DFF // 256
        for blk in range(nblk):
            gs, gv = self.wload(wgu, 0, KC, blk * 256, 256)
            us, uv = self.wload(wgu, 0, KC, DFF + blk * 256, 256)
            pg = [self.psum(), self.psum()]
            pu = [self.psum(), self.psum()]
            for (slot, view, pss) in ((gs, gv, pg), (us, uv, pu)):
                for m in range(2):
                    for k in range(KC):
                        self.mm(pss[m], view[:, k, m * 128:(m + 1) * 128], self.hn[k].ap, k == 0, k == KC - 1,
                                [slot, self.hn[k]])
            aT = [self.pr(2 + 2 * (blk % 2)), self.pr(3 + 2 * (blk % 2))]
            for m in range(2):
                sg = self.pf(1 + (m + 2 * blk) % 4)
                self.act(sg, pg[m], AF.Silu, [pg[m]])
                a = aT[m]
                pum = pu[m]
                self.dve(lambda e, a=a, sg=sg, pum=pum: e.tensor_tensor(out=a.ap, in0=sg.ap, in1=pum.ap, op=ALU.mult),
                         [sg, pum], [a])
            ds, dv = self.wload(wdn, blk * 256, 2, 0, D)
            for c in range(KC):
                po = self.psum()
                for k in range(2):
                    self.mm(po, dv[:, k, c * 128:(c + 1) * 128], aT[k].ap, k == 0, k == 1, [ds, aT[k]])
                x = self.xres[c]
                self.dve(lambda e, x=x, po=po: e.tensor_tensor(out=x.ap, in0=x.ap, in1=po.ap, op=ALU.add),
                         [x, po], [x])


    def softplus_small(self, x, tmp1, tmp2, out, n, neg=False):
        xa, t1, t2, oa = x.ap[:, :n], tmp1.ap[:, :n], tmp2.ap[:, :n], out.ap[:, :n]
        self.act(tmp1, x, AF.Abs, [x], out_ap=t1, in_ap=xa)
        self.act(tmp1, tmp1, AF.Exp, [tmp1], scale=-1.0, out_ap=t1, in_ap=t1)
        self.act(tmp1, tmp1, AF.Ln, [tmp1], bias=1.0, out_ap=t1, in_ap=t1)
        if not neg:
            self.dve(lambda e: e.tensor_scalar_max(out=t2, in0=xa, scalar1=0.0), [x], [tmp2])
            self.dve(lambda e: e.tensor_tensor(out=oa, in0=t2, in1=t1, op=ALU.add), [tmp1, tmp2], [out])
        else:
            self.dve(lambda e: e.tensor_scalar_min(out=t2, in0=xa, scalar1=0.0), [x], [tmp2])
            self.dve(lambda e: e.tensor_tensor(out=oa, in0=t2, in1=t1, op=ALU.subtract), [tmp1, tmp2], [out])

    def conv_chunk(self, ps, cc, dest, dest_ap):
        u = self.ubuf[self.urr % 2]
        acc = self.pf(1 + self.urr % 2)
        self.urr += 1
        cr = self.carry[cc]
        cw, cb = self.convw, self.convb
        self.act(u, ps, AF.Copy, [ps], out_ap=u.ap[:, 3:515])
        self.dve(lambda e: e.tensor_copy(out=u.ap[:, 0:3], in_=cr.ap), [cr], [u])
        w = lambda k: cw.ap[:, cc * 4 + k: cc * 4 + k + 1]
        bcol = cb.ap[:, cc:cc + 1]
        self.dve(lambda e: e.tensor_scalar(out=acc.ap, in0=u.ap[:, 0:512], scalar1=w(0), scalar2=bcol,
                                           op0=ALU.mult, op1=ALU.add), [u, cw, cb], [acc])
        for k in (1, 2, 3):
            self.dve(lambda e, k=k: e.scalar_tensor_tensor(out=acc.ap, in0=u.ap[:, k:k + 512], scalar=w(k),
                                                           in1=acc.ap, op0=ALU.mult, op1=ALU.add), [u, acc, cw], [acc])
        self.dve(lambda e: e.tensor_copy(out=cr.ap, in_=u.ap[:, 512:515]), [u], [cr])
        self.act(dest, acc, AF.Silu, [acc], out_ap=dest_ap)

    def fm_proj_chunk(self, W, c0):
        raise NotImplementedError

    def _inproj_xbc(self, g, xT, BT, CT, need_c):
        for half in range(2):
            sl, wv = self.wload(self.w_in, 0, KC, 4096 + g * 512 + half * 256, 256)
            for mch in range(2):
                i = half * 2 + mch
                ps = self.psum()
                for k in range(KC):
                    self.mm(ps, wv[:, k, mch * 128:(mch + 1) * 128], self.hn[k].ap, k == 0, k == KC - 1, [sl, self.hn[k]])
                self.conv_chunk(ps, g * 4 + i, xT[i], xT[i].ap)
        for (which, dest, cc) in ((0, BT, 32 + g), (1, CT, 40 + g)):
            if which == 1 and not need_c:
                continue
            sl, wv = self.wload(self.w_in, 0, KC, 8192 + which * 1024 + g * 128, 128)
            ps = self.psum()
            for k in range(KC):
                self.mm(ps, wv[:, k, :], self.hn[k].ap, k == 0, k == KC - 1, [sl, self.hn[k]])
            self.conv_chunk(ps, cc, dest, dest.ap)

    def _prefix_fronts(self, g, xT, BT):
        S = self.sstate[g]
        v3 = lambda ap: ap.rearrange("p (h d) -> p h d", d=HP)
        bcg = lambda t: t.ap[:, 8 * g:8 * g + 8].unsqueeze(2).broadcast_to([128, 8, HP])
        psS = self.ps[6 + g % 2]
        for q in range(4):
            m = self.msm[q]
            cs = slice(q * 128, (q + 1) * 128)
            psX = self.psum()
            for i in range(4):
                self.tr(psX, psX.ap[:, i * 128:(i + 1) * 128], xT[i].ap[:, cs], [xT[i]])
            xdt, xdtw = self.pr(4), self.pr(5)
            self.dv("tensor_tensor", [psX, m[0]], [xdt], out=v3(xdt.ap), in0=v3(psX.ap), in1=bcg(m[0]), op=ALU.mult)
            self.dv("tensor_tensor", [xdt, m[5]], [xdtw], out=v3(xdtw.ap), in0=v3(xdt.ap), in1=bcg(m[5]), op=ALU.mult)
            psB = self.psum()
            self.tr(psB, psB.ap[:, 0:128], BT.ap[:, cs], [BT])
            Btok = self.prq(6, q % 4)
            self.act(Btok, psB, AF.Copy, [psB], in_ap=psB.ap[:, 0:128])
            self.mm(psS, Btok.ap, xdtw.ap, q == 0, q == 3, [Btok, xdtw])
        m0 = self.msm[0]
        self.dv("tensor_tensor", [S, m0[6]], [S], out=v3(S.ap), in0=v3(S.ap), in1=bcg(m0[6]), op=ALU.mult)
        self.dv("tensor_tensor", [S, psS], [S], out=S.ap, in0=S.ap, in1=psS.ap, op=ALU.add)

    def mamba_tile(self, prefix=False, need_c=True):
        sq = [self.pr(0), self.pr(1)]
        rstd = self.pf(0)
        self.rmsnorm_to_hn(0, sq, rstd)
        hv = self.hv
        self.ckpt("norm")
        ds, dvw = self.wload(self.w_in, 0, KC, 10240, NH)
        for q in range(4):
            m = self.msm[q]
            ps = self.psum()
            for k in range(KC):
                self.mm(ps, self.hn[k].ap[:, q * 128:(q + 1) * 128], dvw[:, k, :], k == 0, k == KC - 1,
                        [ds, self.hn[k]], out_ap=ps.ap[:, :NH])
            x = m[7]
            self.dve(lambda e, x=x, ps=ps: e.tensor_tensor(out=x.ap, in0=ps.ap[:, :NH], in1=hv.ap[:, 0:NH], op=ALU.add),
                     [ps, hv], [x])
            self.ckpt("dt_mm")
            self.softplus_small(x, m[5], m[6], m[0], NH)
            self.ckpt("dt_sp")
            an = self.Aneg
            self.dve(lambda e, m=m: e.tensor_tensor(out=m[1].ap, in0=m[0].ap, in1=an.ap, op=ALU.mult), [m[0], an], [m[1]])
            ahi, alo = self.msr[q]
            self.dve(lambda e, m=m, ahi=ahi: e.tensor_copy(out=ahi.ap, in_=m[1].ap), [m[1]], [ahi])
            self.dve(lambda e, m=m, ahi=ahi, alo=alo: e.tensor_tensor(out=alo.ap, in0=m[1].ap, in1=ahi.ap.bitcast(F32),
                                                                     op=ALU.subtract), [m[1], ahi], [alo])
            pc, ph, pl = self.psum(), self.psum(), self.psum()
            tr_, onr = self.triu_r, self.ones_r
            self.mm(pc, tr_.ap, ahi.ap, True, False, [tr_, ahi], out_ap=pc.ap[:, :NH])
            self.mm(pc, tr_.ap, alo.ap, False, True, [tr_, alo], out_ap=pc.ap[:, :NH])
            self.mm(ph, tr_.ap, ahi.ap, True, True, [tr_, ahi], out_ap=ph.ap[:, :NH])
            self.mm(pl, onr.ap, ahi.ap, True, False, [onr, ahi], out_ap=pl.ap[:, :NH])
            self.mm(pl, onr.ap, alo.ap, False, True, [onr, alo], out_ap=pl.ap[:, :NH])
            self.dve(lambda e, m=m, pc=pc: e.tensor_copy(out=m[2].ap, in_=pc.ap[:, :NH]), [pc], [m[2]])
            self.dve(lambda e, m=m, ph=ph: e.tensor_scalar(out=m[3].ap, in0=ph.ap[:, :NH], scalar1=-1.0, scalar2=None,
                                                           op0=ALU.mult), [ph], [m[3]])
            self.act(m[4], m[2], AF.Exp, [m[2]])
            self.dve(lambda e, m=m, pl=pl: e.tensor_tensor(out=m[7].ap, in0=pl.ap[:, :NH], in1=m[2].ap, op=ALU.subtract),
                     [pl, m[2]], [m[7]])
            self.act(m[5], m[7], AF.Exp, [m[7]])
            self.dve(lambda e, m=m, pl=pl: e.tensor_copy(out=m[7].ap, in_=pl.ap[:, :NH]), [pl, m[5]], [m[7]])
            self.act(m[6], m[7], AF.Exp, [m[7]])
        if prefix:
            sufs = [Tl(self.af[:, 9, k * NH:(k + 1) * NH], self.afb[9]) for k in range(5)]
            self.dv("memset", [], [sufs[4]], ap=sufs[4].ap, constant=0.0)
            for q in (3, 2, 1, 0):
                mq = self.msm[q]
                self.dv("tensor_tensor", [mq[7], sufs[4]], [sufs[q]], out=sufs[q].ap, in0=mq[7].ap, in1=sufs[4].ap, op=ALU.add)
                self.dv("tensor_tensor", [sufs[4], mq[7]], [sufs[4]], out=sufs[4].ap, in0=sufs[4].ap, in1=mq[7].ap, op=ALU.add)
                self.dv("tensor_tensor", [sufs[q], mq[2]], [sufs[q]], out=sufs[q].ap, in0=sufs[q].ap, in1=mq[2].ap, op=ALU.subtract)
                self.act(mq[5], sufs[q], AF.Exp, [sufs[q]])
            self.act(self.msm[0][6], sufs[4], AF.Exp, [sufs[4]])
        self.ckpt("dt")
        if prefix:
            bufs = [([self.pr(11 + i) for i in range(4)], self.pr(2)), ([self.pr(7 + i) for i in range(4)], self.pr(3))]
            CT = self.pr(15)
            self._inproj_xbc(0, bufs[0][0], bufs[0][1], CT, need_c)
            for g in range(NG):
                if g + 1 < NG:
                    self._inproj_xbc(g + 1, bufs[(g + 1) % 2][0], bufs[(g + 1) % 2][1], CT, need_c)
                self._prefix_fronts(g, bufs[g % 2][0], bufs[g % 2][1])
            return
        for g in range(NG):
            xTg = [self.pr(11 + i) for i in range(4)]
            BTg, CTg = self.pr(2), self.pr(3)
            for half in range(2):
                sl, wv = self.wload(self.w_in, 0, KC, 4096 + g * 512 + half * 256, 256)
                for mch in range(2):
                    i = half * 2 + mch
                    ps = self.psum()
                    for k in range(KC):
                        self.mm(ps, wv[:, k, mch * 128:(mch + 1) * 128], self.hn[k].ap, k == 0, k == KC - 1,
                                [sl, self.hn[k]])
                    self.conv_chunk(ps, g * 4 + i, xTg[i], xTg[i].ap)
            for (which, dest, cc) in ((0, BTg, 32 + g), (1, CTg, 40 + g)):
                if which == 1 and not need_c:
                    continue
                sl, wv = self.wload(self.w_in, 0, KC, 8192 + which * 1024 + g * 128, 128)
                ps = self.psum()
                for k in range(KC):
                    self.mm(ps, wv[:, k, :], self.hn[k].ap, k == 0, k == KC - 1, [sl, self.hn[k]])
                self.conv_chunk(ps, cc, dest, dest.ap)
            self.ckpt("conv")
            sz = [self.pf(3 + q) for q in range(4)]
            if not prefix:
                zps = [self.psum() for q in range(4)]
                for half in range(2):
                    sl, wv = self.wload(self.w_in, 0, KC, g * 512 + half * 256, 256)
                    for q in range(4):
                        for k in range(KC):
                            self.mm(zps[q], self.hn[k].ap[:, q * 128:(q + 1) * 128], wv[:, k, :], k == 0, k == KC - 1,
                                    [sl, self.hn[k]], out_ap=zps[q].ap[:, half * 256:(half + 1) * 256])
                for q in range(4):
                    self.act(sz[q], zps[q], AF.Silu, [zps[q]])
            yTg = [self.pr(7 + i) for i in range(4)]
            S = self.sstate[g]
            v3 = lambda ap: ap.rearrange("p (h d) -> p h d", d=HP)

            def bcg(t, g=g):
                return t.ap[:, 8 * g:8 * g + 8].unsqueeze(2).broadcast_to([128, 8, HP])

            def front(q, g=g, xTg=xTg, BTg=BTg, CTg=CTg, S=S):
                m = self.msm[q]
                cs = slice(q * 128, (q + 1) * 128)
                ctx = {"q": q, "m": m, "cs": cs}
                psX = self.psum()
                for i in range(4):
                    self.tr(psX, psX.ap[:, i * 128:(i + 1) * 128], xTg[i].ap[:, cs], [xTg[i]])
                Xtok, xdt, xdtw = self.pf(7), self.pr(4), self.pr(5)
                if not prefix:
                    dbc = hv.ap[:, 2 * NH + 8 * g: 2 * NH + 8 * g + 8].unsqueeze(2).broadcast_to([128, 8, HP])
                    self.dv("tensor_tensor", [psX, hv], [Xtok], out=v3(Xtok.ap), in0=v3(psX.ap), in1=dbc, op=ALU.mult)
                self.dv("tensor_tensor", [psX, m[0]], [xdt], out=v3(xdt.ap), in0=v3(psX.ap), in1=bcg(m[0]), op=ALU.mult)
                self.pl("tensor_tensor", [xdt, m[5]], [xdtw], out=v3(xdtw.ap), in0=v3(xdt.ap), in1=bcg(m[5]), op=ALU.mult)
                psB = self.psum()
                self.tr(psB, psB.ap[:, 0:128], BTg.ap[:, cs], [BTg])
                Btok = self.prq(6, 0)
                self.act(Btok, psB, AF.Copy, [psB], in_ap=psB.ap[:, 0:128])
                if not prefix:
                    psC = self.psum()
                    self.mm(psC, BTg.ap[:, cs], CTg.ap[:, cs], True, True, [BTg, CTg], out_ap=psC.ap[:, 0:128])
                    CBT = self.pfq(10, 0)
                    self.dv("tensor_tensor", [psC, self.c_all], [CBT], out=CBT.ap, in0=psC.ap[:, 0:128], in1=self.triu, op=ALU.mult)
                    psY0 = self.psum()
                    self.mm(psY0, CTg.ap[:, cs], S.ap, True, True, [CTg, S])
                    y0 = self.pf(8 + q % 2)
                    self.dv("tensor_tensor", [psY0, m[4]], [y0], out=v3(y0.ap), in0=v3(psY0.ap), in1=bcg(m[4]), op=ALU.mult)
                    self.pl("tensor_tensor", [y0, Xtok], [y0], out=y0.ap, in0=y0.ap, in1=Xtok.ap, op=ALU.add)
                    ctx.update(CBT=CBT, y0=y0, xdt=xdt)
                if prefix:
                    psS = self.ps[6]
                    self.mm(psS, Btok.ap, xdtw.ap, q == 0, q == 3, [Btok, xdtw])
                    if q == 3:
                        m0 = self.msm[0]
                        self.dv("tensor_tensor", [S, m0[6]], [S], out=v3(S.ap), in0=v3(S.ap), in1=bcg(m0[6]), op=ALU.mult)
                        self.dv("tensor_tensor", [S, psS], [S], out=S.ap, in0=S.ap, in1=psS.ap, op=ALU.add)
                else:
                    psS = self.psum()
                    self.mm(psS, Btok.ap, xdtw.ap, True, True, [Btok, xdtw])
                    self.pl("tensor_tensor", [S, m[6]], [S], out=v3(S.ap), in0=v3(S.ap), in1=bcg(m[6]), op=ALU.mult)
                    self.dv("tensor_tensor", [S, psS], [S], out=S.ap, in0=S.ap, in1=psS.ap, op=ALU.add)
                return ctx

            def bc_mm(ctx, g=g):
                q = ctx["q"]
                ahi = self.msr[q][0]
                psEb = [self.psum(), self.psum()]
                ctx["psEb"] = psEb
                for hl in range(8):
                    h = 8 * g + hl
                    pe_, c4 = psEb[hl // 4], (hl % 4) * 128
                    self.mm(pe_, ahi.ap[:, h:h + 1].broadcast_to([128, 128]), self.triu_r.ap, True, True,
                            [ahi, self.triu_r], out_ap=pe_.ap[:, c4:c4 + 128])

            def rest(ctx, g=g):
                q, m, CBT, xdt, psEb = ctx["q"], ctx["m"], ctx["CBT"], ctx["xdt"], ctx["psEb"]
                psYd = self.ps[6 + q % 2]
                ctx["psYd"] = psYd
                for hl in range(8):
                    h = 8 * g + hl
                    pe_, c4 = psEb[hl // 4], (hl % 4) * 128
                    E = self.pfq(10, 1 + hl % 3)
                    self.act(E, pe_, AF.Exp, [pe_, m[3]], bias=m[3].ap[:, h:h + 1], in_ap=pe_.ap[:, c4:c4 + 128])
                    GT = self.prq(6, 1 + hl) if hl < 3 else self.prq(16, (hl - 3) % 4)
                    self.dv("scalar_tensor_tensor", [CBT, E], [GT], out=GT.ap, in0=E.ap, scalar=1.0e30, in1=CBT.ap,
                            op0=ALU.min, op1=ALU.mult)
                    self.mm(psYd, GT.ap, xdt.ap[:, hl * HP:(hl + 1) * HP], True, True, [GT, xdt],
                            out_ap=psYd.ap[:, hl * HP:(hl + 1) * HP])

            def tail_dve(ctx, g=g, sz=sz):
                q, y0, psYd = ctx["q"], ctx["y0"], ctx["psYd"]
                self.dv("tensor_tensor", [y0, psYd], [y0], out=y0.ap, in0=y0.ap, in1=psYd.ap, op=ALU.add)
                szq = sz[q]
                self.dv("tensor_tensor", [y0, szq], [y0], out=y0.ap, in0=y0.ap, in1=szq.ap, op=ALU.mult)
                sm = self.small
                self.dv("memset", [], [sm], ap=sm.ap[:, 0:1], constant=0.0)
                y0a, jka, sma = y0.ap, szq.ap, sm.ap
                self.P.add("act", lambda e, y0a=y0a, jka=jka, sma=sma: e.activation(out=jka, in_=y0a, func=AF.Square,
                                                                                  accum_out=sma[:, 0:1]),
                           reads=[y0, sm], writes=[szq, sm])
                self.dv("tensor_scalar", [sm], [sm], out=sm.ap[:, 1:2], in0=sm.ap[:, 0:1], scalar1=1.0 / 512, scalar2=EPS,
                        op0=ALU.mult, op1=ALU.add)
                self.act(sm, sm, AF.Ln, [sm], out_ap=sm.ap[:, 1:2], in_ap=sm.ap[:, 1:2])
                self.act(sm, sm, AF.Exp, [sm], scale=-0.5, out_ap=sm.ap[:, 2:3], in_ap=sm.ap[:, 1:2])
                yn = self.pr(15)
                ctx["yn"] = yn
                self.act(yn, y0, AF.Copy, [y0, sm], scale=sm.ap[:, 2:3])

            def tail_tr(ctx, g=g, yTg=yTg):
                cs, yn = ctx["cs"], ctx["yn"]
                psT = self.psum()
                for i in range(4):
                    self.tr(psT, psT.ap[:, i * 128:(i + 1) * 128], yn.ap[:, i * 128:(i + 1) * 128], [yn])
                gw = self.gnw
                for i in range(4):
                    self.act(yTg[i], psT, AF.Copy, [psT, gw], scale=gw.ap[:, g * 4 + i: g * 4 + i + 1],
                             out_ap=yTg[i].ap[:, cs], in_ap=psT.ap[:, i * 128:(i + 1) * 128])

            if prefix:
                for q in range(4):
                    front(q)
            else:
                cx = front(0)
                bc_mm(cx)
                rest(cx)
                for q in range(1, 4):
                    nx = front(q)
                    bc_mm(nx)
                    tail_dve(cx)
                    rest(nx)
                    tail_tr(cx)
                    cx = nx
                tail_dve(cx)
                tail_tr(cx)
            if not prefix:
                for half in range(2):
                    sl, wv = self.wload(self.w_out, g * 512, 4, half * 1024, 1024)
                    for cc in range(8):
                        po = self.psum()
                        for k in range(4):
                            self.mm(po, wv[:, k, cc * 128:(cc + 1) * 128], yTg[k].ap, k == 0, k == 3, [sl, yTg[k]])
                        x = self.xres[half * 8 + cc]
                        self.dve(lambda e, x=x, po=po: e.tensor_tensor(out=x.ap, in0=x.ap, in1=po.ap, op=ALU.add), [x, po], [x])


    def head_norm(self, psK, h, colap, dest, rk):
        sqk = self.pr(2 + h % 2)
        self.act(sqk, psK, AF.Square, [psK])
        psR = self.psum()
        self.mm(psR, self.ones_r.ap, sqk.ap, True, True, [self.ones_r, sqk])
        self.dv("tensor_scalar", [psR], [rk], out=rk.ap, in0=psR.ap, scalar1=1.0 / AD, scalar2=EPS, op0=ALU.mult, op1=ALU.add)
        self.act(rk, rk, AF.Ln, [rk])
        self.act(rk, rk, AF.Exp, [rk], scale=-0.5)
        self.dv("scalar_tensor_tensor", [psK, rk], [dest], out=dest.ap, in0=psK.ap, scalar=colap, in1=rk.ap,
                op0=ALU.mult, op1=ALU.mult)

    def kvf_tile(self, j):
        sq = [self.pr(0), self.pr(1)]
        rstd = self.pf(0)
        self.rmsnorm_to_hn(2, sq, rstd)
        kq = self.kqnw
        for hp in range(AH // 2):
            sl, wv = self.wload(self.w_kvf, 0, KC, hp * 256, 256)
            for mch in range(2):
                h = 2 * hp + mch
                psK = self.psum()
                for k in range(KC):
                    self.mm(psK, wv[:, k, mch * 128:(mch + 1) * 128], self.hn[k].ap, k == 0, k == KC - 1, [sl, self.hn[k]])
                kout = self.pf(3 + h % 2)
                self.head_norm(psK, h, kq.ap[:, 0:1], kout, self.pf(1 + h % 2))
                self.dma("sp", self.kt_view(j, h, False), kout.ap, [kout], [self.b_kt[j]], f"kt{h % 2}", accum=True)
        vcnt = 0
        for vb in range(8):
            sl, wv = self.wload(self.w_kvf, 0, KC, D + vb * 256, 256)
            for q in range(4):
                psV = self.psum()
                for k in range(KC):
                    self.mm(psV, self.hn[k].ap[:, q * 128:(q + 1) * 128], wv[:, k, :], k == 0, k == KC - 1,
                            [sl, self.hn[k]], out_ap=psV.ap[:, 0:256])
                p, hf = 5 + (vcnt // 2) % 2, vcnt % 2
                vout = Tl(self.af[:, p, hf * 256:(hf + 1) * 256], [self.afb[p][2 * hf], self.afb[p][2 * hf + 1]])
                self.act(vout, psV, AF.Copy, [psV], in_ap=psV.ap[:, 0:256])
                self.dma("sp", self.v_view(j, q // 2, False)[(q % 2) * 128:(q % 2 + 1) * 128, vb * 256:(vb + 1) * 256], vout.ap,
                         [vout], [self.b_v[j]], f"v{vcnt % 4}", accum=True)
                vcnt += 1
        sl, wv = self.wload(self.w_kvf, 0, KC, 2 * D, AH)
        ks = self.ksm
        ct = self.cumtot
        for q in range(4):
            psF = self.psum()
            for k in range(KC):
                self.mm(psF, self.hn[k].ap[:, q * 128:(q + 1) * 128], wv[:, k, :], k == 0, k == KC - 1,
                        [sl, self.hn[k]], out_ap=psF.ap[:, 0:AH])
            self.dv("tensor_tensor", [psF, self.bfv], [ks[0]], out=ks[0].ap, in0=psF.ap[:, 0:AH], in1=self.bfv.ap, op=ALU.add)
            self.softplus_small(ks[0], ks[1], ks[2], ks[3], AH, neg=True)
            lhi, llo = self.msr[q]
            lha, lla = lhi.ap[:, 0:AH], llo.ap[:, 0:AH]
            self.dv("tensor_copy", [ks[3]], [lhi], out=lha, in_=ks[3].ap)
            self.dv("tensor_tensor", [ks[3], lhi], [llo], out=lla, in0=ks[3].ap, in1=lha.bitcast(F32), op=ALU.subtract)
            psC, psT = self.psum(), self.psum()
            tr_, onr = self.triu_r, self.ones_r
            self.mm(psC, tr_.ap, lha, True, False, [tr_, lhi], out_ap=psC.ap[:, 0:AH])
            self.mm(psC, tr_.ap, lla, False, True, [tr_, llo], out_ap=psC.ap[:, 0:AH])
            self.mm(psT, onr.ap, lha, True, False, [onr, lhi], out_ap=psT.ap[:, 0:AH])
            self.mm(psT, onr.ap, lla, False, True, [onr, llo], out_ap=psT.ap[:, 0:AH])
            self.dv("tensor_tensor", [psC, ct], [ks[4]], out=ks[4].ap, in0=psC.ap[:, 0:AH], in1=ct.ap, op=ALU.add)
            self.dv("tensor_tensor", [psT, ct], [ct], out=ct.ap, in0=ct.ap, in1=psT.ap[:, 0:AH], op=ALU.add)
            self.dma("sp", self.cum_view(j, False)[q * 128:(q + 1) * 128, :], ks[4].ap, [ks[4]], [self.b_cum[j]],
                     "cu", accum=True)
            chi = self.prq(4, q)
            self.dv("tensor_copy", [ks[4]], [chi], out=chi.ap[:, 0:AH], in_=ks[4].ap)
            psX = self.psum()
            self.tr(psX, psX.ap[0:AH, 0:128], chi.ap[:, 0:AH], [chi])
            cts = self.cumTsb
            self.act(cts, psX, AF.Copy, [psX], out_ap=cts.ap[:, q * 128:(q + 1) * 128], in_ap=psX.ap[0:AH, 0:128])
        self.dma("sp", self.cumTd[:, j * T:(j + 1) * T], self.cumTsb.ap, [self.cumTsb], [self.b_cumT[j]], "ct")
        groups = [[2 * i, 2 * i + 1] for i in range(self.cfg.get("ncores", 8) // 2)]
        pairs = [(self.packK[j][0], self.gathK[j][0], [self.b_kt[j]]), (self.packK[j][1], self.gathK[j][1], [self.b_kt[j]]),
                 (self.packV[j][0], self.gathV[j][0], [self.b_v[j]]), (self.packV[j][1], self.gathV[j][1], [self.b_v[j]]),
                 (self.packC[j], self.gathC[j], [self.b_cum[j]])]
        for ci, (pk, gt, dep) in enumerate(pairs):
            pk2 = pk.rearrange("(r c) -> r c", r=128)
            gt2 = gt.rearrange("(r c) -> r c", r=256)
            self.P.add("pool", lambda e, pk2=pk2, gt2=gt2: e.collective_compute(
                "AllGather", ALU.bypass, replica_groups=groups, ins=[pk2], outs=[gt2]),
                reads=dep, writes=[self.b_g[j]], stream=f"cc{ci}", inc=1, accum=True)

    def make_nb(self):
        nt = self.NT
        nb = self.pf(10)
        self.nb = nb
        self.nbv = nb.ap[:, 0:4 * nt * AH].rearrange("p (k h) -> p k h", h=AH)
        self.nbpv = nb.ap[:, 256:256 + 4 * nt * AH].rearrange("p (k h) -> p k h", h=AH)
        for j in range(nt):
            self.dma("sp", self.nbv[:, 4 * j:4 * j + 4, :], self.cum_view(j, False).rearrange("(k p) h -> p k h", p=128),
                     [self.b_cum[j]], [nb], "nb", accum=True)
            self.dma("sp", self.nbpv[:, 4 * j:4 * j + 4, :], self.cum_view(j, True).rearrange("(k p) h -> p k h", p=128),
                     [self.b_g[j]], [nb], "nb", accum=True)
        tb, fl = self.totb, self.flags
        last = self.gathC[nt - 1][(T - 1) * AH: T * AH]
        self.dma("sp", tb.ap, last.partition_broadcast(128), [self.b_g[nt - 1]], [tb], "tb")
        self.dv("tensor_scalar", [tb, fl], [tb], out=tb.ap, in0=tb.ap, scalar1=fl.ap[:, 1:2], scalar2=None, op0=ALU.add)
        self.dv("tensor_scalar", [nb], [nb], out=nb.ap, in0=nb.ap, scalar1=-1.0, scalar2=None, op0=ALU.mult)
        self.dv("tensor_tensor", [nb, tb], [nb], out=self.nbpv, in0=self.nbpv,
                in1=tb.ap.unsqueeze(1).broadcast_to([128, 4 * nt, AH]), op=ALU.add)

    def attn_tile(self, j):
        sq = [self.pr(0), self.pr(1)]
        rstd = self.pf(0)
        self.rmsnorm_to_hn(3, sq, rstd)
        qc = self.qcol
        for hp in range(AH // 2):
            sl, wv = self.wload(self.w_q, 0, KC, hp * 256, 256)
            for mch in range(2):
                h = 2 * hp + mch
                psQ = self.psum()
                for k in range(KC):
                    self.mm(psQ, wv[:, k, mch * 128:(mch + 1) * 128], self.hn[k].ap, k == 0, k == KC - 1, [sl, self.hn[k]])
                qout = self.pf(3 + h % 2)
                self.head_norm(psQ, h, qc.ap[:, 0:1], qout, self.pf(1 + h % 2))
                self.dma("sp", self.qTd[h], qout.ap, [qout], [self.b_q[h]], f"q{h % 2}")
        onr, idr, nmr = self.ones_r, self.ident_r, self.negmask_r
        psO, psL = self.ps[6], self.ps[7]
        npv = 4 * self.NT
        nkb = npv + 4 * (j + 1)
        scnt = 0
        ecnt = 0
        for h in range(AH):
            qT = self.pr(2 + h % 2)
            self.dma("pool", qT.ap, self.qTd[h], [self.b_q[h]], [qT], f"lq{h % 2}")
            crow = self.pr(4 + h % 2)
            self.dma("pool", crow.ap, self.cumTd[h, j * T:(j + 1) * T].partition_broadcast(128), [self.b_cumT[j]], [crow],
                     f"lc{h % 2}")
            self.dv("tensor_scalar", [crow], [crow], out=crow.ap, in0=crow.ap, scalar1=1.0 / 128, scalar2=None, op0=ALU.mult)
            items = []
            for sst in range(self.NT + j + 1):
                prev = sst < self.NT
                st = sst if prev else sst - self.NT
                for i in range(4):
                    items.append((sst, prev, st, i))
            stage_tiles = {}

            def emit_s(it):
                nonlocal scnt, ecnt
                sst, prev, st, i = it
                if sst not in stage_tiles:
                    Kst = self.pr(6 + scnt % 4)
                    Vst = self.pr(10 + scnt % 4)
                    sdep = [self.b_g[st]] if prev else [self.b_kt[st], self.b_v[st]]
                    self.dma("pool", Kst.ap, self.kt_view(st, h, prev), sdep, [Kst], f"lk{scnt % 4}")
                    vv = Vst.ap.rearrange("p (i d) -> p i d", d=AD)
                    for vh in range(2):
                        self.dma("pool", vv[:, 2 * vh:2 * vh + 2, :],
                                 self.v_view(st, vh, prev)[:, h * AD:(h + 1) * AD].rearrange("(i p) d -> p i d", p=128),
                                 sdep, [Vst], f"lv{scnt % 4}", accum=(vh == 1))
                    scnt += 1
                    stage_tiles[sst] = (Kst, Vst, vv)
                Kst, Vst, vv = stage_tiles[sst]
                diag = (not prev) and (st == j)
                nbsel = self.nbpv if prev else self.nbv
                c0 = i * 128 if diag else 0
                psS = self.psum()
                self.mm(psS, onr.ap, crow.ap[:, c0:], True, False, [onr, crow], out_ap=psS.ap[:, c0:])
                self.mm(psS, Kst.ap[:, i * 128:(i + 1) * 128], qT.ap[:, c0:], False, not diag, [Kst, qT], out_ap=psS.ap[:, c0:])
                if diag:
                    self.mm(psS, idr.ap, nmr.ap, False, True, [idr, nmr], out_ap=psS.ap[:, c0:c0 + 128])
                ET = self.pr(ecnt % 2)
                ecnt += 1
                self.act(ET, psS, AF.Exp, [psS, self.nb], bias=nbsel[:, 4 * st + i, h:h + 1], out_ap=ET.ap[:, c0:],
                         in_ap=psS.ap[:, c0:])
                return (ET, c0, Vst, vv, i, 4 * sst + i)

            esum = self.pf(6)

            def emit_pv(p):
                ET, c0, Vst, vv, i, kb = p
                self.mm(psO, vv[:, i, :], ET.ap[:, c0:], kb == 0, kb == nkb - 1, [Vst, ET], out_ap=psO.ap[:, c0:])
                if kb == 0:
                    self.dv("tensor_copy", [ET], [esum], out=esum.ap, in_=ET.ap.bitcast(F32))
                else:
                    self.dv("tensor_tensor", [ET, esum], [esum], out=esum.ap[:, c0:], in0=esum.ap[:, c0:],
                            in1=ET.ap[:, c0:].bitcast(F32), op=ALU.add)

            pend = None
            for it in items:
                cur = emit_s(it)
                if pend is not None:
                    emit_pv(pend)
                pend = cur
            emit_pv(pend)
            esr = self.pr(14)
            self.dv("tensor_copy", [esum], [esr], out=esr.ap, in_=esum.ap)
            self.mm(psL, onr.ap, esr.ap, True, True, [onr, esr])
            rl = self.pf(5)
            self.dv("reciprocal", [psL], [rl], out=rl.ap, in_=psL.ap)
            oT = self.hn[h]
            self.dv("tensor_tensor", [psO, rl], [oT], out=oT.ap, in0=psO.ap, in1=rl.ap, op=ALU.mult)
        for cb in range(8):
            sl, wv = self.wload(self.w_o, 0, KC, cb * 256, 256)
            for mch in range(2):
                po = self.psum()
                for k in range(KC):
                    self.mm(po, wv[:, k, mch * 128:(mch + 1) * 128], self.hn[k].ap, k == 0, k == KC - 1, [sl, self.hn[k]])
                x = self.xres[2 * cb + mch]
                self.dv("tensor_tensor", [x, po], [x], out=x.ap, in0=x.ap, in1=po.ap, op=ALU.add)

    def load_tile(self, src, j, dep=None):
        for c in range(KC):
            x = self.xres[c]
            self.dma("sp", x.ap, src[c, :, j * T:(j + 1) * T], [dep] if dep else [], [x], f"x{c}")

    def store_tile(self, dst, j, tag=None):
        for c in range(KC):
            x = self.xres[c]
            self.dma("sp", dst[c, :, j * T:(j + 1) * T], x.ap, [x], [tag] if tag else [], f"o{c}", accum=True)


def _build(S, cfg):
    B = Builder(S, cfg)
    B.load_consts()
    mode = cfg.get("mode", "full")
    if mode == "full":
        nt = cfg.get("ntiles", B.NT)
        for j in range(nt if cfg.get("prefix_pass", True) else 0):
            B.load_tile(B.xpT, j)
            B.mamba_tile(prefix=True, need_c=(j == nt - 1))
        fl = B.flags
        for g in range(NG):
            st_ = B.sstate[g]
            B.dv("tensor_scalar", [st_, fl], [st_], out=st_.ap, in0=st_.ap, scalar1=fl.ap[:, 0:1], scalar2=None, op0=ALU.mult)
        for c in range(48):
            cr = B.carry[c]
            B.dv("tensor_scalar", [cr, fl], [cr], out=cr.ap, in0=cr.ap, scalar1=fl.ap[:, 0:1], scalar2=None, op0=ALU.mult)
        for j in range(nt):
            B.load_tile(B.xT, j)
            B.mamba_tile()
            B.ffn(0)
            B.kvf_tile(j)
            B.store_tile(B.h1T, j, tag=B.b_h1[j])
        B.make_nb()
        for j in range(nt if cfg.get("sweep2", True) else 0):
            B.load_tile(B.h1T, j, dep=B.b_h1[j])
            if cfg.get("attn", True):
                B.attn_tile(j)
            if cfg.get("ffn1", True):
                B.ffn(1)
            B.store_tile(B.outT, j)
    if mode == "mamba_only":
        for j in range(cfg.get("ntiles", 1)):
            B.load_tile(B.xT, j)
            try:
                B.mamba_tile(prefix=cfg.get("prefix", False))
            except StopBuild:
                pass
            B.store_tile(B.outT, j)
    if mode == "ffn_only":
        for j in range(cfg.get("ntiles", 1)):
            B.load_tile(B.xT, j)
            B.ffn(0)
            B.store_tile(B.outT, j)
    B.P.emit()
    return B


def _layout_inputs(inp):
    f = np.float32
    col = lambda v: np.ascontiguousarray(np.asarray(v, f).reshape(-1, 128).T)
    rep = lambda v: np.ascontiguousarray(np.broadcast_to(np.asarray(v, f).reshape(1, -1), (128, np.asarray(v).size)))
    normw = np.concatenate([col(inp["a_norm_w"][0]), col(inp["ffn_norm_w"][0]), col(inp["kv_norm_w"]),
                            col(inp["b_norm_w"][0]), col(inp["ffn_norm_w"][1])], axis=1)
    cw = np.asarray(inp["a_conv_w"][0], f)
    convw = np.ascontiguousarray(cw.reshape(4, 48, 128).transpose(2, 1, 0).reshape(128, 48 * 4))
    ident = np.eye(128, dtype=f)
    triu = np.triu(np.ones((128, 128), f))
    negmask = (np.tril(np.ones((128, 128), f), -1) * NEG).astype(f)
    shared = {
        "w_in": np.ascontiguousarray(inp["a_in_proj"][0], f), "w_out": np.ascontiguousarray(inp["a_out_proj"][0], f),
        "w_gu0": np.ascontiguousarray(inp["w_gate_up"][0], f), "w_gu1": np.ascontiguousarray(inp["w_gate_up"][1], f),
        "w_dn0": np.ascontiguousarray(inp["w_down"][0], f), "w_dn1": np.ascontiguousarray(inp["w_down"][1], f),
        "w_kvf": np.ascontiguousarray(inp["w_kvf"], f), "w_q": np.ascontiguousarray(inp["w_q"][0], f),
        "w_o": np.ascontiguousarray(inp["w_o"][0], f),
        "normw": np.ascontiguousarray(normw), "gnw": col(inp["a_gnorm_w"][0]),
        "kqnw": np.ascontiguousarray(np.stack([np.asarray(inp["k_norm_w"], f), np.asarray(inp["q_norm_w"][0], f)], axis=1)),
        "convw": convw, "convb": col(inp["a_conv_b"][0]),
        "hvec": np.concatenate([rep(inp["a_dt_bias"][0]), rep(inp["a_A_log"][0]), rep(inp["a_D"][0])], axis=1),
        "bfv": rep(inp["b_f"]),
        "consts": np.concatenate([ident, triu, negmask], axis=1),
    }
    return shared


def _xT(xb):
    S = xb.shape[0]
    return np.ascontiguousarray(np.asarray(xb, np.float32).T.reshape(KC, 128, S))


def kernel(**inputs):
    x = np.asarray(inputs["x"], np.float32)
    nb, S, _ = x.shape
    H = S // 2
    shared = _layout_inputs(inputs)
    B = _build(H, {"mode": "full"})
    in_maps = []
    for b in range(nb):
        first = _xT(x[b, :H])
        for half in range(2):
            m = dict(shared)
            m["xT"] = first if half == 0 else _xT(x[b, H:])
            m["xpT"] = first
            fl = np.zeros((128, 2), np.float32)
            fl[:, 0] = float(half)
            fl[:, 1] = 0.0 if half == 1 else NEG
            m["flags"] = fl
            in_maps.append(m)
    res = run_bass_kernel_spmd(B.nc, in_maps, core_ids=list(range(2 * nb)))
    out = np.empty((nb, S, D), np.float32)
    for b in range(nb):
        for half in range(2):
            o = res.results[2 * b + half]["outT"]
            out[b, half * H:(half + 1) * H] = o.reshape(D, H).T
    return out
```
